# Optimizing a Trainium2 kernel written in Bass

```python
import jax
import jax.numpy as jnp
from jax import lax
import numpy as np

D_MODEL = 2048
BATCH = 4
SEQ = 4096
DEPTH = 4

GRID_W = 64
CTX_LEN = 256
EPS = 1e-6
N_BRANCH = 4
BR_W = 512
NA_HEADS = 4
NA_HEAD_DIM = BR_W // NA_HEADS
NA_ROWS = 8
NA_COLS = 16
POOL_WINDOWS = (2, 4, 8, 16)
POOL_GROUP = BR_W // len(POOL_WINDOWS)
MLA_HEADS = 4
MLA_NOPE = 128
MLA_ROPE = 64
MLA_V = BR_W // MLA_HEADS
MLA_Q_LORA = 512
MLA_KV_LORA = 512
ROPE_THETA = 10000.0
ATTN_BLOCK = 128
ML_HEADS = 4
ML_HEAD_DIM = BR_W // ML_HEADS
ML_CHUNK = 64
CONV_K = 4

IN_SPLITS = (
    ('a_qkv', 3 * BR_W), ('a_gate', BR_W),
    ('b_in', BR_W), ('b_gate', BR_W),
    ('c_q', MLA_Q_LORA), ('c_kv', MLA_KV_LORA), ('c_kr', MLA_ROPE), ('c_gate', BR_W),
    ('d_qkv', 3 * BR_W), ('d_o', BR_W), ('d_if', 4 * ML_HEADS), ('d_gate', BR_W),
    ('merge', N_BRANCH * D_MODEL),
)
IN_NAMES = tuple(name for name, _ in IN_SPLITS)
IN_OFFSETS = tuple(int(o) for o in np.cumsum([w for _, w in IN_SPLITS])[:-1])
D_IN = sum(w for _, w in IN_SPLITS)

kernel_name = 'hybrid_gated_branch_diffusion_trunk'

F32 = jnp.float32


def rmsnorm(x, g):
    xf = x.astype(F32)
    y = xf * lax.rsqrt(jnp.mean(xf * xf, axis=-1, keepdims=True) + EPS)
    return (y * g.astype(F32)).astype(x.dtype)


def split_heads(t, h):
    b, n, _ = t.shape
    return t.reshape(b, n, h, -1).transpose(0, 2, 1, 3)


def merge_heads(t):
    b, h, n, d = t.shape
    return t.transpose(0, 2, 1, 3).reshape(b, n, h * d)


def modulate_project(h, cond, norm_g, w_mod, b_mod, w_in):
    mod = jax.nn.silu(cond) @ w_mod + b_mod
    shift, scale, gate = jnp.split(mod, 3, axis=-1)
    hn = rmsnorm(h, norm_g) * (1 + scale) + shift
    parts = jnp.split(hn @ w_in, list(IN_OFFSETS), axis=-1)
    return dict(zip(IN_NAMES, parts)), gate


def dense_attention(q, k, v, scale):
    s = jnp.einsum('bhqd,bhkd->bhqk', q, k).astype(F32) * scale
    p = jax.nn.softmax(s, axis=-1).astype(v.dtype)
    return jnp.einsum('bhqk,bhkd->bhqd', p, v)


def blocked_joint_attention(q_a, q_b, k_a, v_a, k_b, v_b, scale):
    b, h, t, _ = q_a.shape
    nb = t // ATTN_BLOCK

    def to_blocks(a):
        return a.reshape(b, h, nb, ATTN_BLOCK, a.shape[-1]).transpose(2, 0, 1, 3, 4)

    def one_block(args):
        qa, qb = args
        s = jnp.concatenate([jnp.einsum('bhqd,bhkd->bhqk', qa, k_a),
                             jnp.einsum('bhqd,bhkd->bhqk', qb, k_b)], axis=-1).astype(F32) * scale
        p = jax.nn.softmax(s, axis=-1).astype(v_a.dtype)
        return (jnp.einsum('bhqk,bhkd->bhqd', p[..., :t], v_a)
                + jnp.einsum('bhqk,bhkd->bhqd', p[..., t:], v_b))

    out = lax.map(one_block, (to_blocks(q_a), to_blocks(q_b)))
    return out.transpose(1, 2, 0, 3, 4).reshape(b, h, t, -1)


def neighbourhood_attention(q, k, v, kc, vc, rpb):
    b, h, t, dh = q.shape
    rows = t // GRID_W
    kr = min(NA_ROWS, rows)
    scale = dh ** -0.5
    qg = q.reshape(b, h, rows, GRID_W, dh)
    r = jnp.arange(rows)
    ridx = jnp.clip(r - kr // 2, 0, rows - kr)[:, None] + jnp.arange(kr)[None, :]
    kb = jnp.take(k.reshape(b, h, rows, GRID_W, dh), ridx, axis=2)
    vb = jnp.take(v.reshape(b, h, rows, GRID_W, dh), ridx, axis=2)
    cq = jnp.arange(GRID_W)
    cstart = jnp.clip(cq - NA_COLS // 2, 0, GRID_W - NA_COLS)
    valid = (cq[None, :] >= cstart[:, None]) & (cq[None, :] < cstart[:, None] + NA_COLS)
    roff = ridx - r[:, None] + NA_ROWS - 1
    coff = jnp.clip(cq[None, :] - cq[:, None] + NA_COLS - 1, 0, 2 * NA_COLS - 2)
    bias = rpb[:, roff[:, None, :, None], coff[None, :, None, :]].astype(F32)
    s_loc = jnp.einsum('bhrqd,bhrkcd->bhrqkc', qg, kb).astype(F32) * scale + bias[None]
    s_loc = jnp.where(valid[:, None, :], s_loc, -jnp.inf)
    s_ctx = jnp.einsum('bhrqd,bhnd->bhrqn', qg, kc).astype(F32) * scale
    n_loc = kr * GRID_W
    p = jax.nn.softmax(jnp.concatenate([s_loc.reshape(b, h, rows, GRID_W, n_loc), s_ctx], axis=-1),
                       axis=-1).astype(v.dtype)
    out = (jnp.einsum('bhrqkc,bhrkcd->bhrqd', p[..., :n_loc].reshape(b, h, rows, GRID_W, kr, GRID_W), vb)
           + jnp.einsum('bhrqn,bhnd->bhrqd', p[..., n_loc:], vc))
    return out.reshape(b, h, t, dh)


def mixer_neighbourhood(pl, pc, rpb, need_ctx):
    qc, kc, vc = (split_heads(a, NA_HEADS) for a in jnp.split(pc['a_qkv'], 3, axis=-1))
    q, k, v = (split_heads(a, NA_HEADS) for a in jnp.split(pl['a_qkv'], 3, axis=-1))
    y = merge_heads(neighbourhood_attention(q, k, v, kc, vc, rpb)) * jax.nn.silu(pl['a_gate'])
    y_c = None
    if need_ctx:
        y_c = merge_heads(dense_attention(qc, kc, vc, NA_HEAD_DIM ** -0.5)) * jax.nn.silu(pc['a_gate'])
    return y, y_c


def multiscale_pool_diff(u):
    b, n, ch = u.shape
    uf = u.astype(F32)
    csum = jnp.concatenate([jnp.zeros((b, 1, ch), F32), jnp.cumsum(uf, axis=1)], axis=1)
    t = jnp.arange(n)
    outs = []
    for gi, w in enumerate(POOL_WINDOWS):
        sl = slice(gi * POOL_GROUP, (gi + 1) * POOL_GROUP)
        lo = jnp.maximum(t - w // 2, 0)
        hi = jnp.minimum(t + (w - 1 - w // 2), n - 1)
        cnt = (hi - lo + 1).astype(F32)[:, None]
        mean = (csum[:, hi + 1, sl] - csum[:, lo, sl]) / cnt
        outs.append(mean - uf[:, :, sl])
    return jnp.concatenate(outs, axis=-1)


def pool_branch(p, pool_w, pool_scale):
    u = p['b_in']
    b, n, _ = u.shape
    d = multiscale_pool_diff(u).reshape(b, n, len(POOL_WINDOWS), POOL_GROUP)
    y = jnp.einsum('bngi,gio->bngo', d, pool_w.astype(F32)).reshape(b, n, BR_W)
    y = (y * pool_scale.astype(F32)).astype(u.dtype)
    return y * jax.nn.silu(p['b_gate'])


def mixer_pool(pl, pc, pool_w, pool_scale, need_ctx):
    y = pool_branch(pl, pool_w, pool_scale)
    y_c = pool_branch(pc, pool_w, pool_scale) if need_ctx else None
    return y, y_c


def rope_axis(x, pos):
    n = x.shape[-1] // 2
    inv = ROPE_THETA ** (-jnp.arange(n, dtype=F32) / n)
    ang = pos.astype(F32)[:, None] * inv
    cos, sin = jnp.cos(ang), jnp.sin(ang)
    x1, x2 = x[..., :n].astype(F32), x[..., n:].astype(F32)
    return jnp.concatenate([x1 * cos - x2 * sin, x1 * sin + x2 * cos], axis=-1)


def axial_rope(x, rows, cols):
    half = x.shape[-1] // 2
    out = jnp.concatenate([rope_axis(x[..., :half], rows), rope_axis(x[..., half:], cols)], axis=-1)
    return out.astype(x.dtype)


def mla_project(p, gq, gkv, w_uq, w_ukv):
    b, n, _ = p['c_q'].shape
    q = (rmsnorm(p['c_q'], gq) @ w_uq).reshape(b, n, MLA_HEADS, MLA_NOPE + MLA_ROPE).transpose(0, 2, 1, 3)
    kv = (rmsnorm(p['c_kv'], gkv) @ w_ukv).reshape(b, n, MLA_HEADS, MLA_NOPE + MLA_V).transpose(0, 2, 1, 3)
    k_rope = p['c_kr'][:, None]
    return q[..., :MLA_NOPE], q[..., MLA_NOPE:], kv[..., :MLA_NOPE], kv[..., MLA_NOPE:], k_rope


def mixer_mla(pl, pc, gq, gkv, w_uq, w_ukv, need_ctx):
    scale = (MLA_NOPE + MLA_ROPE) ** -0.5
    qn_c, qr_c, kn_c, v_c, kr_c = mla_project(pc, gq, gkv, w_uq, w_ukv)
    k_ctx = jnp.concatenate([kn_c, jnp.broadcast_to(kr_c, kn_c.shape[:-1] + (MLA_ROPE,))], axis=-1)
    qn, qr, kn, v, kr = mla_project(pl, gq, gkv, w_uq, w_ukv)
    t = qn.shape[2]
    pos = jnp.arange(t)
    rows, cols = pos // GRID_W, pos % GRID_W
    q_lat = jnp.concatenate([qn, axial_rope(qr, rows, cols)], axis=-1)
    q_for_ctx = jnp.concatenate([qn, qr], axis=-1)
    k_lat = jnp.concatenate([kn, jnp.broadcast_to(axial_rope(kr, rows, cols), kn.shape[:-1] + (MLA_ROPE,))], axis=-1)
    y = merge_heads(blocked_joint_attention(q_lat, q_for_ctx, k_lat, v, k_ctx, v_c, scale)) * jax.nn.silu(pl['c_gate'])
    y_c = None
    if need_ctx:
        q_c = jnp.concatenate([qn_c, qr_c], axis=-1)
        y_c = merge_heads(dense_attention(q_c, k_ctx, v_c, scale)) * jax.nn.silu(pc['c_gate'])
    return y, y_c


def short_conv(x, w):
    n = x.shape[1]
    xp = jnp.pad(x, ((0, 0), (CONV_K // 2, CONV_K - 1 - CONV_K // 2), (0, 0)))
    acc = xp[:, 0:n] * w[0]
    for j in range(1, CONV_K):
        acc = acc + xp[:, j:j + n] * w[j]
    return acc


def mlstm_inputs(p, conv_w, b_if):
    b, n, _ = p['d_qkv'].shape
    q_raw, k_raw, v = jnp.split(p['d_qkv'], 3, axis=-1)
    q, k = jnp.split(jax.nn.silu(short_conv(jnp.concatenate([q_raw, k_raw], axis=-1), conv_w)), 2, axis=-1)
    q = split_heads(q, ML_HEADS).astype(F32)
    k = split_heads(k, ML_HEADS).astype(F32) * (ML_HEAD_DIM ** -0.5)
    v = split_heads(v, ML_HEADS).astype(F32)
    g = (p['d_if'].astype(F32) + b_if.astype(F32)).reshape(b, n, 2, 2, ML_HEADS).transpose(2, 3, 0, 4, 1)
    li = g[:, 0]
    lf = jax.nn.log_sigmoid(g[:, 1])
    return q, k, v, li, lf


def mlstm_scan(q, k, v, li, lf, state):
    b, h, n, _ = q.shape
    nc = n // ML_CHUNK
    tril = jnp.tril(jnp.ones((ML_CHUNK, ML_CHUNK), dtype=bool))

    def chunks(a):
        return jnp.moveaxis(a.reshape(a.shape[:2] + (nc, ML_CHUNK) + a.shape[3:]), 2, 0)

    def step(carry, xs):
        C, nv, m = carry
        qc, kc, vc, lic, lfc = xs
        bc = jnp.cumsum(lfc, axis=-1)
        dmat = jnp.where(tril, bc[..., :, None] - bc[..., None, :] + lic[..., None, :], -jnp.inf)
        inter = bc + m[..., None]
        m_t = jnp.maximum(inter, jnp.max(dmat, axis=-1))
        s = jnp.einsum('bhtd,bhsd->bhts', qc, kc) * jnp.exp(dmat - m_t[..., None])
        a = jnp.exp(inter - m_t)
        num = a[..., None] * jnp.einsum('bhtd,bhde->bhte', qc, C) + jnp.einsum('bhts,bhse->bhte', s, vc)
        den = a * jnp.einsum('bhtd,bhd->bht', qc, nv) + jnp.sum(s, axis=-1)
        h_out = num / jnp.maximum(jnp.abs(den), jnp.exp(-m_t))[..., None]
        b_last = bc[..., -1]
        gk = b_last[..., None] - bc + lic
        m_new = jnp.maximum(b_last + m, jnp.max(gk, axis=-1))
        wk = jnp.exp(gk - m_new[..., None])
        decay = jnp.exp(b_last + m - m_new)
        C_new = decay[..., None, None] * C + jnp.einsum('bhsd,bhse->bhde', kc * wk[..., None], vc)
        n_new = decay[..., None] * nv + jnp.einsum('bhs,bhsd->bhd', wk, kc)
        return (C_new, n_new, m_new), h_out

    state, hs = lax.scan(step, state, tuple(chunks(a) for a in (q, k, v, li, lf)))
    return jnp.moveaxis(hs, 0, 2).reshape(b, h, n, -1), state


def mlstm_bidirectional(q, k, v, li, lf, states):
    h_f, s_f = mlstm_scan(q, k, v, li[0], lf[0], states[0])
    flip = lambda a: jnp.flip(a, axis=2)
    h_b, s_b = mlstm_scan(flip(q), flip(k), flip(v), flip(li[1]), flip(lf[1]), states[1])
    return h_f + flip(h_b), (s_f, s_b)


def mlstm_output(h, p, ml_gnorm):
    hn = h * lax.rsqrt(jnp.mean(h * h, axis=-1, keepdims=True) + EPS)
    y = merge_heads(hn) * ml_gnorm.astype(F32) * jax.nn.sigmoid(p['d_o'].astype(F32))
    return y.astype(p['d_gate'].dtype) * jax.nn.silu(p['d_gate'])


def mixer_mlstm(pl, pc, conv_w, b_if, ml_gnorm, need_ctx):
    q_c, k_c, v_c, li_c, lf_c = mlstm_inputs(pc, conv_w, b_if)
    b = q_c.shape[0]
    zero = (jnp.zeros((b, ML_HEADS, ML_HEAD_DIM, ML_HEAD_DIM), F32),
            jnp.zeros((b, ML_HEADS, ML_HEAD_DIM), F32),
            jnp.zeros((b, ML_HEADS), F32))
    h_c, ctx_states = mlstm_bidirectional(q_c, k_c, v_c, li_c, lf_c, (zero, zero))
    q, k, v, li, lf = mlstm_inputs(pl, conv_w, b_if)
    h, _ = mlstm_bidirectional(q, k, v, li, lf, ctx_states)
    y = mlstm_output(h, pl, ml_gnorm)
    y_c = mlstm_output(h_c, pc, ml_gnorm) if need_ctx else None
    return y, y_c


def merge_branches(merge_pre, ys, w_br, w_out):
    b, n, _ = merge_pre.shape
    g = jax.nn.sigmoid(merge_pre.reshape(b, n, N_BRANCH, D_MODEL))
    acc = g[:, :, 0] * (ys[0] @ w_br[0])
    for i in range(1, N_BRANCH):
        acc = acc + g[:, :, i] * (ys[i] @ w_br[i])
    return acc @ w_out


def hybrid_layer(x, xc, c, c_ctx, norm_g, w_mod, b_mod, w_in, na_rpb, pool_w, pool_scale,
                 mla_gq, mla_gkv, w_uq, w_ukv, conv_w, b_if, ml_gnorm, w_br, w_out, need_ctx):
    pl, gate = modulate_project(x, c[:, None, :], norm_g, w_mod, b_mod, w_in)
    pc, gate_c = modulate_project(xc, c_ctx[None, None, :], norm_g, w_mod, b_mod, w_in)
    ya, ya_c = mixer_neighbourhood(pl, pc, na_rpb, need_ctx)
    yb, yb_c = mixer_pool(pl, pc, pool_w, pool_scale, need_ctx)
    ym, ym_c = mixer_mla(pl, pc, mla_gq, mla_gkv, w_uq, w_ukv, need_ctx)
    yd, yd_c = mixer_mlstm(pl, pc, conv_w, b_if, ml_gnorm, need_ctx)
    x = x + gate * merge_branches(pl['merge'], (ya, yb, ym, yd), w_br, w_out)
    if need_ctx:
        xc = xc + gate_c * merge_branches(pc['merge'], (ya_c, yb_c, ym_c, yd_c), w_br, w_out)
    return x, xc


def setup_inputs(seed: int = 0) -> dict:
    key = jax.random.key(seed)
    ks = jax.random.split(key, 24)
    L = DEPTH

    def nrm(k, shape, s):
        return jax.random.normal(k, shape, F32) * s

    b_i = nrm(ks[8], (L, 2, 1, ML_HEADS), 0.1)
    b_f = jax.random.uniform(ks[9], (L, 2, 1, ML_HEADS), F32, 3.0, 6.0)
    return {
        'x': nrm(ks[0], (BATCH, SEQ, D_MODEL), 1.0),
        'c': nrm(ks[1], (BATCH, D_MODEL), 1.0),
        'ctx': nrm(ks[2], (BATCH, CTX_LEN, D_MODEL), 1.0),
        'c_ctx': nrm(ks[3], (D_MODEL,), 1.0),
        'norm_g': 1.0 + nrm(ks[4], (L, D_MODEL), 0.1),
        'w_mod': nrm(ks[5], (L, D_MODEL, 3 * D_MODEL), 0.5 * D_MODEL ** -0.5),
        'b_mod': nrm(ks[6], (L, 3 * D_MODEL), 0.02),
        'w_in': nrm(ks[7], (L, D_MODEL, D_IN), D_MODEL ** -0.5),
        'na_rpb': nrm(ks[10], (L, NA_HEADS, 2 * NA_ROWS - 1, 2 * NA_COLS - 1), 0.5),
        'pool_w': nrm(ks[11], (L, len(POOL_WINDOWS), POOL_GROUP, POOL_GROUP), POOL_GROUP ** -0.5),
        'pool_scale': 1.0 + nrm(ks[12], (L, BR_W), 0.1),
        'mla_gq': 1.0 + nrm(ks[13], (L, MLA_Q_LORA), 0.1),
        'mla_gkv': 1.0 + nrm(ks[14], (L, MLA_KV_LORA), 0.1),
        'w_uq': nrm(ks[15], (L, MLA_Q_LORA, MLA_HEADS * (MLA_NOPE + MLA_ROPE)), MLA_Q_LORA ** -0.5),
        'w_ukv': nrm(ks[16], (L, MLA_KV_LORA, MLA_HEADS * (MLA_NOPE + MLA_V)), MLA_KV_LORA ** -0.5),
        'conv_w': nrm(ks[17], (L, CONV_K, 2 * BR_W), CONV_K ** -0.5),
        'b_if': jnp.concatenate([b_i, b_f], axis=2).reshape(L, 4 * ML_HEADS),
        'ml_gnorm': 1.0 + nrm(ks[18], (L, BR_W), 0.1),
        'w_br': nrm(ks[19], (L, N_BRANCH, BR_W, D_MODEL), BR_W ** -0.5),
        'w_out': nrm(ks[20], (L, D_MODEL, D_MODEL), D_MODEL ** -0.5),
        'g_final': 1.0 + nrm(ks[21], (D_MODEL,), 0.1),
    }


def reference(x, c, ctx, c_ctx, norm_g, w_mod, b_mod, w_in, na_rpb, pool_w, pool_scale,
              mla_gq, mla_gkv, w_uq, w_ukv, conv_w, b_if, ml_gnorm, w_br, w_out, g_final):
    xc = ctx
    for l in range(DEPTH):
        x, xc = hybrid_layer(x, xc, c, c_ctx, norm_g[l], w_mod[l], b_mod[l], w_in[l], na_rpb[l],
                             pool_w[l], pool_scale[l], mla_gq[l], mla_gkv[l], w_uq[l], w_ukv[l],
                             conv_w[l], b_if[l], ml_gnorm[l], w_br[l], w_out[l],
                             need_ctx=(l < DEPTH - 1))
    return rmsnorm(x, g_final)
```

```python
import numpy as np
import ml_dtypes
from contextlib import ExitStack
import concourse.bass as bass
import concourse.mybir as mybir
from concourse.bass_utils import run_bass_kernel_spmd

F32 = mybir.dt.float32
BF16 = mybir.dt.bfloat16
AF = mybir.ActivationFunctionType
ALU = mybir.AluOpType

D = 2048
NT = 4352
NCTX = 256
NLAT = 4096
KC = 16
DEPTH = 4
EPS = 1e-6
BR = 512

O_AQKV, O_AG, O_BIN, O_BG, O_CQ, O_CKV, O_CKR, O_CG, O_DQKV, O_DO, O_DIF, O_DG, O_MERGE = (
    0, 1536, 2048, 2560, 3072, 3584, 4096, 4160, 4672, 6208, 6720, 6736, 7248)


def _rope_perm():
    p = np.arange(64)
    half = p // 32
    j = p % 32
    return half * 32 + (j + 16) % 32


def _units():
    u = []
    r = np.arange
    for h in range(4):
        u.append(('a_q%d' % h, O_AQKV + h * 128 + r(128), 'bf'))
    for h in range(4):
        u.append(('a_k%d' % h, O_AQKV + 512 + h * 128 + r(128), 'bf'))
    for h in range(4):
        u.append(('a_v%d' % h, O_AQKV + 1024 + h * 128 + r(128), 'bf'))
    for h in range(4):
        u.append(('d_v%d' % h, O_DQKV + 1024 + h * 128 + r(128), 'bf'))
    for i in range(4):
        u.append(('b_in%d' % i, O_BIN + i * 128 + r(128), 'f'))
    for i in range(4):
        u.append(('c_q%d' % i, O_CQ + i * 128 + r(128), 'f'))
    for i in range(4):
        u.append(('c_kv%d' % i, O_CKV + i * 128 + r(128), 'f'))
    u.append(('c_kr', np.concatenate([O_CKR + r(64), O_CKR + _rope_perm()]), 'f'))
    for i in range(4):
        u.append(('d_q%d' % i, O_DQKV + i * 128 + r(128), 'f'))
    for i in range(4):
        u.append(('d_k%d' % i, O_DQKV + 512 + i * 128 + r(128), 'f'))
    u.append(('d_if', O_DIF + r(16), 'f'))
    for nm, o in (('a_g', O_AG), ('b_g', O_BG), ('c_g', O_CG), ('d_g', O_DG)):
        for i in range(4):
            u.append(('%s%d' % (nm, i), o + i * 128 + r(128), 'silu'))
    for i in range(4):
        u.append(('d_o%d' % i, O_DO + i * 128 + r(128), 'sig'))
    return u


UNITS = _units()
NU = len(UNITS)
UIDX = {n: i for i, (n, _, _) in enumerate(UNITS)}
NBF_U = 16


def token_groups(lo, hi, w):
    out = []
    t = lo
    while t < hi:
        lim = NCTX if t < NCTX else hi
        ww = min(w, lim - t, hi - t)
        out.append((t, ww))
        t += ww
    return out


class Sem:
    __slots__ = ('h', 'n')

    def __init__(self, h):
        self.h = h
        self.n = 0


class Dep:
    __slots__ = ('w', 'r')

    def __init__(self):
        self.w = {}
        self.r = {}


class KB:
    def __init__(self, nc, st, n_dma_sems=40):
        self.nc = nc
        self.eng = {'pe': nc.tensor, 'act': nc.scalar, 'dve': nc.vector, 'pool': nc.gpsimd, 'sp': nc.sync}
        self.esem = {}
        for e in self.eng:
            self.esem[e] = Sem(st.enter_context(nc.semaphore('es_' + e)))
        self.seen = {e: {} for e in self.eng}
        self.free_dsems = [Sem(st.enter_context(nc.semaphore('ds%d' % i))) for i in range(n_dma_sems)]
        self.uid = 0
        self.ninst = {e: 0 for e in self.eng}

    def name(self, p):
        self.uid += 1
        return '%s_%d' % (p, self.uid)

    def dsem(self):
        return self.free_dsems.pop()

    def rel_dsem(self, s):
        self.free_dsems.append(s)

    def _waits(self, e, reads, writes):
        own = self.esem.get(e)
        need = {}
        for d in reads:
            for s, v in d.w.items():
                if v > need.get(s, 0):
                    need[s] = v
        for d in writes:
            for s, v in d.w.items():
                if s is own:
                    continue
                if v > need.get(s, 0):
                    need[s] = v
            for s, v in d.r.items():
                if s is own:
                    continue
                if v > need.get(s, 0):
                    need[s] = v
        seen = self.seen[e]
        for s, v in need.items():
            if seen.get(s, 0) >= v:
                continue
            self.eng[e].wait_ge(s.h, v)
            seen[s] = v
            self.ninst[e] += 1

    def op(self, e, fn, reads=(), writes=(), inc=True):
        self._waits(e, reads, writes)
        own = self.esem[e]
        ins = fn(self.eng[e])
        self.ninst[e] += 1
        if inc:
            own.n += 1
            ins.then_inc(own.h, 1)
            val = own.n
        else:
            val = own.n + 1
        for d in reads:
            if d.r.get(own, 0) < val:
                d.r[own] = val
        for d in writes:
            d.w[own] = val
        return ins

    def dma(self, q, out, in_, sem, reads=(), writes=()):
        self._waits(q, reads, writes)
        ins = self.eng[q].dma_start(out=out, in_=in_)
        self.ninst[q] += 1
        sem.n += 16
        ins.then_inc(sem.h, 16)
        for d in reads:
            d.r[sem] = sem.n
        for d in writes:
            d.w[sem] = sem.n
        return ins

    def wait_all(self, e, deps):
        self._waits(e, deps, deps)


class Tile:
    def __init__(self, kb, st, name, shape, dtype, slots=1, psum=False, dma=False):
        nc = kb.nc
        full = [shape[0], slots] + list(shape[1:])
        alloc = nc.psum_tensor if psum else nc.sbuf_tensor
        self.t = st.enter_context(alloc(kb.name(name), full, dtype))
        self.slots = slots
        self.deps = [Dep() for _ in range(slots)]
        self.sems = None
        self.kb = kb
        if dma:
            self.sems = [kb.dsem() for _ in range(slots)]
            st.callback(lambda: [kb.rel_dsem(s) for s in self.sems])
        self.i = -1
        st.callback(lambda: [kb.wait_all(e, self.deps) for e in kb.eng])

    def nxt(self):
        self.i = (self.i + 1) % self.slots
        return self.i

    def ap(self, s=0):
        return self.t[:, s]


def build(cfg):
    L = cfg.get('layers', DEPTH)
    dump = set(cfg.get('dump', ()))
    stub_y = cfg.get('stub_y', False)
    do_mix = cfg.get('mixers', ('a', 'b', 'c', 'd'))
    nc = bass.Bass("TRN2", target_bir_lowering=False)
    WL = cfg.get('wlayers', DEPTH)

    def dram(name, shape, dt, kind=None):
        if kind is None:
            kind = "ExternalOutput" if name in dump else "Internal"
        return nc.dram_tensor(name, list(shape), dt, kind=kind).ap()

    x_in = dram("x", [NLAT, D], F32, "ExternalInput")
    ctx_in = dram("ctx", [NCTX, D], F32, "ExternalInput")
    ccT_in = dram("ccT", [128, KC, 2], F32, "ExternalInput")
    w_mod = dram("w_mod", [WL, D, 3 * D], F32, "ExternalInput")
    b_modT = dram("b_modT", [WL, 128, 48], F32, "ExternalInput")
    norm_gT = dram("norm_gT", [WL, 128, KC], F32, "ExternalInput")
    w_in_r = dram("w_in_r", [WL, D, NU * 128], F32, "ExternalInput")
    w_in_m = dram("w_in_m", [WL, D, 4 * D], F32, "ExternalInput")
    w_br = dram("w_br", [WL, 4, BR, D], F32, "ExternalInput")
    w_out = dram("w_out", [WL, D, D], F32, "ExternalInput")
    g_finalT = dram("g_finalT", [128, KC], F32, "ExternalInput")
    consts = dram("consts", [128, 3, 128], F32, "ExternalInput")
    invc = dram("invc", [128, 4, NT], F32, "ExternalInput")
    pool_w = dram("pool_w", [WL, 4, 128, 128], F32, "ExternalInput")
    vecs2 = dram("vecs2", [WL, 128, 52], F32, "ExternalInput")
    w_uq_ext = dram("w_uq_ext", [WL, 512, 1280], F32, "ExternalInput")
    w_ukv = dram("w_ukv", [WL, 512, 1024], F32, "ExternalInput")
    ropeT = dram("ropeT", [64, 2, NLAT], F32, "ExternalInput")
    na_bias = dram("na_bias", [WL, 4, 128, 21, 128], F32, "ExternalInput")
    dmask = dram("dmask", [128, 8, 512], BF16, "ExternalInput")
    sel_in = dram("sel", [128, 4, 128], F32, "ExternalInput")
    MQ = dram("MQ", [1024, NT], BF16)
    MK = dram("MK", [576, NT], BF16)
    MV = dram("MV", [512, NT], BF16)
    QD = dram("QD", [512, NT], BF16)
    KD = dram("KD", [512, NT], BF16)
    GS = dram("GS", [2, 3, 4, NT], F32)
    DBG = dram("DBG", [4, 4, NT], F32)
    if stub_y:
        y_stub = dram("y_stub", [D, NT], BF16, "ExternalInput")
    out_d = dram("out", [NLAT, D], F32, "ExternalOutput")

    XT = dram("XT", [D, NT], F32)
    HN = dram("HN", [D, NT], BF16)
    PB = dram("PB", [NBF_U * 128, NT], BF16)
    PF = dram("PF", [(NU - NBF_U) * 128, NT], F32)
    YT = dram("YT", [D, NT], BF16)
    ACCT = dram("ACCT", [D, NT], BF16)

    def fm(ap):
        return ap.rearrange("(c p) t -> p c t", p=128)

    with ExitStack() as gst:
        kb = KB(nc, gst)
        cst = Tile(kb, gst, "cst", [128, 3, 128], F32, dma=True)
        cst_bf = Tile(kb, gst, "cstbf", [128, 3, 128], BF16)
        epsT = Tile(kb, gst, "eps", [128, 1], F32)
        banks = [Tile(kb, gst, "bank%d" % i, [128, 512], F32, psum=True) for i in range(7)]
        bankT = Tile(kb, gst, "bankT", [128, 1024], BF16, psum=True)
        modv = Tile(kb, gst, "modv", [128, 48, 2], F32)
        g1 = Tile(kb, gst, "g1", [128, KC, 2], F32)
        vecs = Tile(kb, gst, "vecs", [128, KC + 48 + KC], F32, dma=True)
        ccT = Tile(kb, gst, "ccT", [128, KC, 2], F32, dma=True)
        sccT = Tile(kb, gst, "sccT", [128, KC, 2], F32)

        kb.dma('sp', cst.ap(), consts, cst.sems[0], writes=[cst.deps[0]])
        kb.op('dve', lambda e: e.tensor_copy(cst_bf.ap(), cst.ap()), reads=[cst.deps[0]], writes=[cst_bf.deps[0]])
        kb.op('dve', lambda e: e.memset(epsT.ap(), EPS), writes=[epsT.deps[0]])
        ident_f = cst.t[:, 0, 0, :]
        ones_f = cst.t[:, 0, 1, :]
        ident_b = cst_bf.t[:, 0, 0, :]
        ones_b = cst_bf.t[:, 0, 1, :]
        CD = [cst.deps[0]]
        CBD = [cst_bf.deps[0]]
        kb.dma('sp', ccT.ap(), ccT_in, ccT.sems[0], writes=[ccT.deps[0]])
        kb.op('act', lambda e: e.activation(sccT.ap(), ccT.ap(), AF.Silu), reads=[ccT.deps[0]], writes=[sccT.deps[0]])

        bank_i = [0]

        def next_bank():
            b = banks[bank_i[0] % 7]
            bank_i[0] += 1
            return b

        NPART = 4
        PART = NT // NPART
        XT_ds = [[Dep() for _ in range(NPART)] for _ in range(KC)]

        def xtd(t0, t1, us=None):
            ps = range(t0 // PART, (t1 - 1) // PART + 1)
            return [XT_ds[u][p] for u in (range(KC) if us is None else us) for p in ps]
        HN_d = Dep()
        PB_d = Dep()
        PF_d = Dep()
        YT_d = Dep()
        ACCT_d = Dep()

        with ExitStack() as st:
            xin = Tile(kb, st, "xin", [128, D], F32, slots=2, dma=True)
            xo = Tile(kb, st, "xo", [128, KC, 128], F32, slots=2, dma=True)
            for ti in range(NT // 128):
                s = xin.nxt()
                src = ctx_in[ti * 128:(ti + 1) * 128, :] if ti < 2 else x_in[(ti - 2) * 128:(ti - 1) * 128, :]
                kb.dma('sp', xin.ap(s), src, xin.sems[s], writes=[xin.deps[s]])
                so = xo.nxt()
                for q in range(4):
                    bk = next_bank()
                    for j in range(4):
                        c = q * 4 + j
                        kb.op('pe', lambda e, c=c, j=j, bk=bk, s=s: e.transpose(
                            bk.t[:, 0, j * 128:(j + 1) * 128], xin.t[:, s, c * 128:(c + 1) * 128], ident_f),
                            reads=[xin.deps[s]] + CD, writes=[bk.deps[0]], inc=(j == 3))
                    eng = 'dve' if q % 2 == 0 else 'act'
                    if eng == 'dve':
                        kb.op('dve', lambda e, q=q, bk=bk, so=so: e.tensor_copy(
                            xo.t[:, so, q * 4:(q + 1) * 4, :], bk.t[:, 0, :].rearrange("p (a b) -> p a b", a=4)),
                            reads=[bk.deps[0]], writes=[xo.deps[so]])
                    else:
                        kb.op('act', lambda e, q=q, bk=bk, so=so: e.activation(
                            xo.t[:, so, q * 4:(q + 1) * 4, :], bk.t[:, 0, :].rearrange("p (a b) -> p a b", a=4), AF.Copy),
                            reads=[bk.deps[0]], writes=[xo.deps[so]])
                kb.dma('sp', fm(XT)[:, :, ti * 128:(ti + 1) * 128], xo.ap(so), xo.sems[so],
                       reads=[xo.deps[so]], writes=xtd(ti * 128, (ti + 1) * 128))

        def stream_units(st, n_units, src_fn, kch, body, depth=2, pre=None):
            wst = Tile(kb, st, "wst", [128, kch, 128], F32, slots=3, dma=True)
            wbf = Tile(kb, st, "wbf", [128, kch, 128], BF16, slots=3)
            slots = {}

            def load(u):
                s = wst.nxt()
                kb.dma('sp', wst.ap(s), src_fn(u), wst.sems[s], writes=[wst.deps[s]])
                s2 = wbf.nxt()
                kb.op('pool', lambda e: e.tensor_copy(wbf.ap(s2), wst.ap(s)), reads=[wst.deps[s]], writes=[wbf.deps[s2]])
                slots[u] = s2
                if pre is not None:
                    pre(u)
            for u in range(min(depth, n_units)):
                load(u)
            for u in range(n_units):
                s2 = slots.pop(u)
                body(u, wbf.t[:, s2], wbf.deps[s2])
                if u + depth < n_units:
                    load(u + depth)

        stb = banks[0:3]
        numb, denb, mb1, mb2 = banks[3], banks[4], banks[5], banks[6]
        st_i = [0]
        SC_NA = 128.0 ** -0.5
        SC_MLA = 192.0 ** -0.5
        SC_ML = 128.0 ** -0.5
        MQ_d, MK_d, MV_d, QD_d, KD_d, GS_d = Dep(), Dep(), Dep(), Dep(), Dep(), Dep()

        def pf_rows(name, n=128, r0=0):
            u = UIDX[name]
            if u < NBF_U:
                return PB[u * 128 + r0:u * 128 + r0 + n, :]
            return PF[(u - NBF_U) * 128 + r0:(u - NBF_U) * 128 + r0 + n, :]

        def make_V(st, src_rows, src_dep):
            vT = Tile(kb, st, "vT", [128, NT], BF16, dma=True)
            V = Tile(kb, st, "V", [128, 34, 128], BF16)
            kb.dma('sp', vT.ap(), src_rows, vT.sems[0], reads=[src_dep], writes=[vT.deps[0]])
            for b0 in range(0, 34, 8):
                nb = min(8, 34 - b0)
                for j in range(nb):
                    kb.op('pe', lambda e: e.transpose(bankT.t[:, 0, j * 128:(j + 1) * 128],
                                                      vT.t[:, 0, (b0 + j) * 128:(b0 + j + 1) * 128], ident_b),
                          reads=[vT.deps[0]] + CBD, writes=[bankT.deps[0]], inc=(j == nb - 1))
                kb.op('dve', lambda e: e.tensor_copy(V.t[:, 0, b0:b0 + nb, :],
                                                     bankT.t[:, 0, 0:nb * 128].rearrange("p (a b) -> p a b", a=nb)),
                      reads=[bankT.deps[0]], writes=[V.deps[0]])
            return V

        def attn_core(items, W):
            n = len(items)
            sts = [None] * n

            def issue_st(i):
                sb = stb[st_i[0] % 3]
                st_i[0] += 1
                mm = items[i]['mm']
                for q, (lt, rh, dp) in enumerate(mm):
                    kb.op('pe', lambda e: e.matmul(sb.t[:, 0, 0:W], lt, rh, start=(q == 0), stop=(q == len(mm) - 1)),
                          reads=dp, writes=[sb.deps[0]], inc=(q == len(mm) - 1))
                sts[i] = sb
            issue_st(0)
            if n > 1:
                issue_st(1)
            for i in range(n):
                E, Ed = items[i]['post'](sts[i])
                if i + 2 < n:
                    issue_st(i + 2)
                kb.op('pe', lambda e: e.matmul(numb.t[:, 0, 0:W], items[i]['v'], E, start=(i == 0), stop=(i == n - 1)),
                      reads=[Ed] + items[i]['vd'], writes=[numb.deps[0]], inc=False)
                kb.op('pe', lambda e: e.matmul(denb.t[:, 0, 0:W], ones_b, E, start=(i == 0), stop=(i == n - 1)),
                      reads=[Ed] + CBD, writes=[denb.deps[0]], inc=True)

        def mixers(l):
            with ExitStack() as mst:
                v2 = Tile(kb, mst, "v2", [128, 52], F32, dma=True)
                kb.dma('sp', v2.ap(), vecs2[l], v2.sems[0], writes=[v2.deps[0]])
                Et = Tile(kb, mst, "Et", [128, 512], BF16, slots=4)
                rec = Tile(kb, mst, "rec", [128, 512], F32, slots=2)
                y1 = Tile(kb, mst, "y1", [128, 512], F32, slots=2)
                gt = Tile(kb, mst, "gt", [128, 512], F32, slots=2, dma=True)
                yst = Tile(kb, mst, "yst", [128, 512], BF16, slots=2, dma=True)

                def exp_post(scale):
                    def post(sb, W):
                        s = Et.nxt()
                        kb.op('act', lambda e: e.activation(Et.t[:, s, 0:W], sb.t[:, 0, 0:W], AF.Exp, scale=scale),
                              reads=[sb.deps[0]], writes=[Et.deps[s]])
                        return Et.t[:, s, 0:W], Et.deps[s]
                    return post

                def finalize_softmax(W, gname, yrow0, t0, ycol=None):
                    r = rec.nxt()
                    kb.op('dve', lambda e: e.reciprocal(rec.t[:, r, 0:W], denb.t[:, 0, 0:W]),
                          reads=[denb.deps[0]], writes=[rec.deps[r]])
                    a = y1.nxt()
                    kb.op('dve', lambda e: e.tensor_tensor(y1.t[:, a, 0:W], numb.t[:, 0, 0:W], rec.t[:, r, 0:W], ALU.mult),
                          reads=[numb.deps[0], rec.deps[r]], writes=[y1.deps[a]])
                    return a

                if 'b' in do_mix:
                    with ExitStack() as st:
                        PADW = 16
                        SEG = [(0, NCTX, 0), (NCTX, NLAT, NCTX + 2 * PADW)]
                        TOTW = NT + 4 * PADW
                        up = Tile(kb, st, "up", [128, TOTW], F32, slots=2, dma=True)
                        la = Tile(kb, st, "la", [128, TOTW], F32)
                        lb = Tile(kb, st, "lb", [128, TOTW], F32)
                        ic = Tile(kb, st, "ic", [128, NT], F32, dma=True)
                        dd = Tile(kb, st, "dd", [128, NT], BF16)
                        pw = Tile(kb, st, "pw", [128, 128], F32, slots=2, dma=True)
                        pwb = Tile(kb, st, "pwb", [128, 128], BF16, slots=2)
                        for s in range(2):
                            kb.op('pool', lambda e: e.memset(up.ap(s), 0.0), writes=[up.deps[s]])
                        for gi, wdw in enumerate((2, 4, 8, 16)):
                            s = up.nxt()
                            for (tk, n, po) in SEG:
                                kb.dma('sp', up.t[:, s, po + PADW:po + PADW + n], pf_rows('b_in%d' % gi)[:, tk:tk + n], up.sems[s],
                                       reads=[PF_d], writes=[up.deps[s]])
                            kb.dma('sp', ic.ap(), invc[:, gi, :], ic.sems[0], writes=[ic.deps[0]])
                            ps = pw.nxt()
                            kb.dma('sp', pw.ap(ps), pool_w[l, gi], pw.sems[ps], writes=[pw.deps[ps]])
                            kb.op('pool', lambda e: e.tensor_copy(pwb.ap(ps), pw.ap(ps)), reads=[pw.deps[ps]], writes=[pwb.deps[ps]])
                            cur, curd = up.t[:, s], up.deps[s]
                            k = 1
                            tl = [la, lb]
                            ti = 0
                            while k < wdw:
                                dst = tl[ti % 2]
                                ti += 1
                                lo = 2 * k - 1
                                kb.op('dve', lambda e: e.tensor_tensor(dst.t[:, 0, lo:TOTW], cur[:, lo:TOTW], cur[:, lo - k:TOTW - k], ALU.add),
                                      reads=[curd], writes=[dst.deps[0]])
                                cur, curd = dst.t[:, 0], dst.deps[0]
                                k *= 2
                            oth = tl[ti % 2]
                            for (tk, n, po) in SEG:
                                sh = po + PADW + wdw // 2 - 1
                                kb.op('dve', lambda e: e.tensor_tensor(oth.t[:, 0, 0:n], cur[:, sh:sh + n], ic.t[:, 0, tk:tk + n], ALU.mult),
                                      reads=[curd, ic.deps[0]], writes=[oth.deps[0]])
                                kb.op('dve', lambda e: e.tensor_tensor(dd.t[:, 0, tk:tk + n], oth.t[:, 0, 0:n], up.t[:, s, po + PADW:po + PADW + n], ALU.subtract),
                                      reads=[oth.deps[0], up.deps[s]], writes=[dd.deps[0]])
                            for (t0, W) in token_groups(0, NT, 512):
                                bk = next_bank()
                                kb.op('pe', lambda e: e.matmul(bk.t[:, 0, 0:W], pwb.ap(ps), dd.t[:, 0, t0:t0 + W], start=True, stop=True),
                                      reads=[pwb.deps[ps], dd.deps[0]], writes=[bk.deps[0]])
                                g = gt.nxt()
                                kb.dma('sp', gt.t[:, g, 0:W], pf_rows('b_g%d' % gi)[:, t0:t0 + W], gt.sems[g], reads=[PF_d], writes=[gt.deps[g]])
                                ys = yst.nxt()
                                kb.op('dve', lambda e: e.scalar_tensor_tensor(yst.t[:, ys, 0:W], bk.t[:, 0, 0:W], v2.t[:, 0, gi:gi + 1],
                                                                              gt.t[:, g, 0:W], ALU.mult, ALU.mult),
                                      reads=[bk.deps[0], v2.deps[0], gt.deps[g]], writes=[yst.deps[ys]])
                                kb.dma('sp', YT[512 + gi * 128:512 + (gi + 1) * 128, t0:t0 + W], yst.t[:, ys, 0:W], yst.sems[ys],
                                       reads=[yst.deps[ys]], writes=[YT_d])

                if 'a' in do_mix:
                    for h in range(4):
                        with ExitStack() as st:
                            qh = Tile(kb, st, "naq", [128, NT], BF16, dma=True)
                            kh = Tile(kb, st, "nak", [128, NT], BF16, dma=True)
                            nb = Tile(kb, st, "nab", [128, 21, 128], F32, dma=True)
                            tmpb = Tile(kb, st, "natmp", [128, 128], F32, slots=3)
                            kb.dma('sp', qh.ap(), pf_rows('a_q%d' % h), qh.sems[0], reads=[PB_d], writes=[qh.deps[0]])
                            kb.dma('sp', kh.ap(), pf_rows('a_k%d' % h), kh.sems[0], reads=[PB_d], writes=[kh.deps[0]])
                            kb.dma('sp', nb.ap(), na_bias[l, h], nb.sems[0], writes=[nb.deps[0]])
                            V = make_V(st, pf_rows('a_v%d' % h), PB_d)
                            ep = exp_post(SC_NA)

                            def bias_post(idx):
                                def post(sb, W):
                                    q = tmpb.nxt()
                                    kb.op('dve', lambda e: e.scalar_tensor_tensor(tmpb.t[:, q, :], sb.t[:, 0, 0:128], SC_NA, nb.t[:, 0, idx, :],
                                                                                  ALU.mult, ALU.add),
                                          reads=[sb.deps[0], nb.deps[0]], writes=[tmpb.deps[q]])
                                    s = Et.nxt()
                                    kb.op('act', lambda e: e.activation(Et.t[:, s, 0:128], tmpb.t[:, q, :], AF.Exp),
                                          reads=[tmpb.deps[q]], writes=[Et.deps[s]])
                                    return Et.t[:, s, 0:128], Et.deps[s]
                                return post

                            def item(kbk, q0, W, post):
                                return dict(mm=[(kh.t[:, 0, kbk * 128:(kbk + 1) * 128], qh.t[:, 0, q0:q0 + W], [kh.deps[0], qh.deps[0]])],
                                            post=lambda sb: post(sb, W), v=V.t[:, 0, kbk, :], vd=[V.deps[0]])
                            attn_core([item(0, 0, 256, ep), item(1, 0, 256, ep)], 256)
                            a = finalize_softmax(256, None, None, 0)
                            g = gt.nxt()
                            kb.dma('sp', gt.t[:, g, 0:256], pf_rows('a_g%d' % h)[:, 0:256], gt.sems[g], reads=[PF_d], writes=[gt.deps[g]])
                            ys = yst.nxt()
                            kb.op('pool', lambda e: e.tensor_tensor(yst.t[:, ys, 0:256], y1.t[:, a, 0:256], gt.t[:, g, 0:256], ALU.mult),
                                  reads=[y1.deps[a], gt.deps[g]], writes=[yst.deps[ys]])
                            kb.dma('sp', YT[h * 128:(h + 1) * 128, 0:256], yst.t[:, ys, 0:256], yst.sems[ys], reads=[yst.deps[ys]], writes=[YT_d])
                            for t in range(32):
                                q0 = NCTX + t * 128
                                if t == 0:
                                    dl, base = range(0, 4), 5
                                elif t == 1:
                                    dl, base = range(-1, 3), 9
                                elif t == 30:
                                    dl, base = range(-2, 2), 13
                                elif t == 31:
                                    dl, base = range(-3, 1), 17
                                else:
                                    dl, base = range(-2, 3), 0
                                items = [item(0, q0, 128, ep), item(1, q0, 128, ep)]
                                for ii, dlt in enumerate(dl):
                                    items.append(item(2 + t + dlt, q0, 128, bias_post(base + ii)))
                                attn_core(items, 128)
                                a = finalize_softmax(128, None, None, q0)
                                if t % 4 == 0:
                                    g = gt.nxt()
                                    kb.dma('sp', gt.t[:, g, :], pf_rows('a_g%d' % h)[:, q0:q0 + 512], gt.sems[g], reads=[PF_d], writes=[gt.deps[g]])
                                    ys = yst.nxt()
                                c0 = (t % 4) * 128
                                kb.op('pool', lambda e: e.tensor_tensor(yst.t[:, ys, c0:c0 + 128], y1.t[:, a, 0:128], gt.t[:, g, c0:c0 + 128], ALU.mult),
                                      reads=[y1.deps[a], gt.deps[g]], writes=[yst.deps[ys]])
                                if t % 4 == 3:
                                    kb.dma('sp', YT[h * 128:(h + 1) * 128, q0 - 384:q0 + 128], yst.t[:, ys, :], yst.sems[ys],
                                           reads=[yst.deps[ys]], writes=[YT_d])

                if 'c' in do_mix:
                    with ExitStack() as st:
                        wq = Tile(kb, st, "wq", [128, 4, 1280], BF16)
                        wkv = Tile(kb, st, "wkv", [128, 4, 1024], BF16)
                        wstg = Tile(kb, st, "wstg", [128, 4, 640], F32, slots=2, dma=True)
                        for wi, (src, dstt) in enumerate(((w_uq_ext, wq), (w_ukv, wkv))):
                            srcv = src[l].rearrange("(k p) n -> p k n", p=128)
                            pc = 640 if wi == 0 else 512
                            for hh in range(2):
                                s = wstg.nxt()
                                kb.dma('sp', wstg.t[:, s, :, 0:pc], srcv[:, :, hh * pc:(hh + 1) * pc], wstg.sems[s], writes=[wstg.deps[s]])
                                kb.op('pool', lambda e: e.tensor_copy(dstt.t[:, 0, :, hh * pc:(hh + 1) * pc], wstg.t[:, s, :, 0:pc]),
                                      reads=[wstg.deps[s]], writes=[dstt.deps[0]])
                        cin = Tile(kb, st, "cin", [128, 4, 512], F32, slots=2, dma=True)
                        cn = Tile(kb, st, "cn", [128, 4, 512], BF16, slots=2)
                        sq = Tile(kb, st, "csq", [128, 512], F32, slots=3)
                        rs = Tile(kb, st, "crs", [128, 512], F32, slots=2)
                        rsd = Tile(kb, st, "crsd", [128, 512], F32, slots=2)
                        ctmp = Tile(kb, st, "ctmp", [128, 512], F32, slots=2)
                        krt = Tile(kb, st, "krt", [64, 2, 512], F32, slots=2, dma=True)
                        rope = Tile(kb, st, "rope", [64, 2, 512], F32, slots=2, dma=True)
                        r1 = Tile(kb, st, "r1", [64, 512], F32, slots=2)
                        r2 = Tile(kb, st, "r2", [64, 512], F32, slots=2)
                        ob = Tile(kb, st, "cob", [128, 512], BF16, slots=4, dma=True)

                        def store(rows_ap, src_fn, M, W, dep):
                            o = ob.nxt()
                            src_fn(ob.t[0:M, o, 0:W], ob.deps[o])
                            kb.dma('sp', rows_ap, ob.t[0:M, o, 0:W], ob.sems[o], reads=[ob.deps[o]], writes=[dep])

                        def proj(wt, c0, M, rhs_t, rs_, W):
                            bk = next_bank()
                            for k in range(4):
                                kb.op('pe', lambda e: e.matmul(bk.t[0:M, 0, 0:W], wt.t[:, 0, k, c0:c0 + M], rhs_t.t[:, rs_, k, 0:W],
                                                               start=(k == 0), stop=(k == 3)),
                                      reads=[wt.deps[0], rhs_t.deps[rs_]], writes=[bk.deps[0]], inc=(k == 3))
                            return bk

                        def copy_to(bk, M, W, eng='dve'):
                            def f(out_ap, odep):
                                if eng == 'dve':
                                    kb.op('dve', lambda e: e.tensor_copy(out_ap, bk.t[0:M, 0, 0:W]), reads=[bk.deps[0]], writes=[odep])
                                else:
                                    kb.op('act', lambda e: e.activation(out_ap, bk.t[0:M, 0, 0:W], AF.Copy), reads=[bk.deps[0]], writes=[odep])
                            return f

                        def rope_to(A_ap, A_dep, B_ap, B_dep, rp, W):
                            def f(out_ap, odep):
                                a = r1.nxt()
                                kb.op('dve', lambda e: e.tensor_tensor(r1.t[:, a, 0:W], A_ap, rope.t[:, rp, 0, 0:W], ALU.mult),
                                      reads=[A_dep, rope.deps[rp]], writes=[r1.deps[a]])
                                b = r2.nxt()
                                kb.op('dve', lambda e: e.tensor_tensor(r2.t[:, b, 0:W], B_ap, rope.t[:, rp, 1, 0:W], ALU.mult),
                                      reads=[B_dep, rope.deps[rp]], writes=[r2.deps[b]])
                                kb.op('pool', lambda e: e.tensor_tensor(out_ap, r1.t[:, a, 0:W], r2.t[:, b, 0:W], ALU.add),
                                      reads=[r1.deps[a], r2.deps[b]], writes=[odep])
                            return f

                        c1cut = cfg.get('c1_cut', 99)
                        for (t0, W) in (token_groups(0, NT, 512) if c1cut >= 1 else []):
                            lat = (t0 >= NCTX) and not cfg.get('no_rope', False)
                            if lat:
                                rp = rope.nxt()
                                kb.dma('sp', rope.t[:, rp, :, 0:W], ropeT[:, :, t0 - NCTX:t0 - NCTX + W], rope.sems[rp], writes=[rope.deps[rp]])
                            normed = []
                            for which, gcol in (('c_q', 4), ('c_kv', 8)):
                                ci = cin.nxt()
                                for c in range(4):
                                    kb.dma('sp', cin.t[:, ci, c, 0:W], pf_rows('%s%d' % (which, c))[:, t0:t0 + W], cin.sems[ci],
                                           reads=[PF_d], writes=[cin.deps[ci]])
                                bk = next_bank()
                                for c in range(4):
                                    q = sq.nxt()
                                    kb.op('act', lambda e: e.activation(sq.t[:, q, 0:W], cin.t[:, ci, c, 0:W], AF.Square),
                                          reads=[cin.deps[ci]], writes=[sq.deps[q]])
                                    kb.op('pe', lambda e: e.matmul(bk.t[:, 0, 0:W], ones_f, sq.t[:, q, 0:W], start=(c == 0), stop=(c == 3)),
                                          reads=[sq.deps[q]] + CD, writes=[bk.deps[0]], inc=True)
                                r = rs.nxt()
                                kb.op('act', lambda e: e.activation(rs.t[:, r, 0:W], bk.t[:, 0, 0:W], AF.Sqrt, bias=epsT.t[:, 0, :], scale=1.0 / 512),
                                      reads=[bk.deps[0], epsT.deps[0]], writes=[rs.deps[r]])
                                rd = rsd.nxt()
                                kb.op('dve', lambda e: e.reciprocal(rsd.t[:, rd, 0:W], rs.t[:, r, 0:W]), reads=[rs.deps[r]], writes=[rsd.deps[rd]])
                                ni = cn.nxt()
                                for c in range(4):
                                    kb.op('dve', lambda e: e.scalar_tensor_tensor(cn.t[:, ni, c, 0:W], cin.t[:, ci, c, 0:W], v2.t[:, 0, gcol + c:gcol + c + 1],
                                                                                  rsd.t[:, rd, 0:W], ALU.mult, ALU.mult),
                                          reads=[cin.deps[ci], v2.deps[0], rsd.deps[rd]], writes=[cn.deps[ni]])
                                normed.append(ni)
                            nq, nkv = normed
                            for h in (range(4) if c1cut >= 2 else []):
                                bk = proj(wq, h * 320, 128, cn, nq, W)
                                store(MQ[h * 256:h * 256 + 128, t0:t0 + W], copy_to(bk, 128, W, 'act'), 128, W, MQ_d)
                                if c1cut < 3:
                                    continue
                                bA = proj(wq, h * 320 + 128, 128, cn, nq, W)
                                bB = proj(wq, h * 320 + 192, 128, cn, nq, W)
                                store(MQ[h * 256 + 192:h * 256 + 256, t0:t0 + W], copy_to(bA, 64, W, 'dve'), 64, W, MQ_d)
                                if lat:
                                    store(MQ[h * 256 + 128:h * 256 + 192, t0:t0 + W],
                                          rope_to(bA.t[0:64, 0, 0:W], bA.deps[0], bB.t[0:64, 0, 0:W], bB.deps[0], rp, W), 64, W, MQ_d)
                                else:
                                    store(MQ[h * 256 + 128:h * 256 + 192, t0:t0 + W], copy_to(bA, 64, W, 'dve'), 64, W, MQ_d)
                                bk = proj(wkv, h * 256, 128, cn, nkv, W)
                                store(MK[h * 128:(h + 1) * 128, t0:t0 + W], copy_to(bk, 128, W, 'act'), 128, W, MK_d)
                                bk = proj(wkv, h * 256 + 128, 128, cn, nkv, W)
                                store(MV[h * 128:(h + 1) * 128, t0:t0 + W], copy_to(bk, 128, W, 'dve'), 128, W, MV_d)
                            if c1cut < 4:
                                continue
                            ki = krt.nxt()
                            kb.dma('sp', krt.t[:, ki, 0, 0:W], pf_rows('c_kr', 64, 0)[:, t0:t0 + W], krt.sems[ki], reads=[PF_d], writes=[krt.deps[ki]])
                            kb.dma('sp', krt.t[:, ki, 1, 0:W], pf_rows('c_kr', 64, 64)[:, t0:t0 + W], krt.sems[ki], reads=[PF_d], writes=[krt.deps[ki]])
                            if lat:
                                store(MK[512:576, t0:t0 + W], rope_to(krt.t[:, ki, 0, 0:W], krt.deps[ki], krt.t[:, ki, 1, 0:W], krt.deps[ki], rp, W), 64, W, MK_d)
                            else:
                                def cp(out_ap, odep):
                                    kb.op('dve', lambda e: e.tensor_copy(out_ap, krt.t[:, ki, 0, 0:W]), reads=[krt.deps[ki]], writes=[odep])
                                store(MK[512:576, t0:t0 + W], cp, 64, W, MK_d)
                    with ExitStack() as st:
                      if cfg.get('c_stage', 2) >= 2:
                          krr = Tile(kb, st, "krr", [128, NT], BF16, dma=True)
                          kb.op('pool', lambda e: e.memset(krr.ap(), 0.0), writes=[krr.deps[0]])
                          kb.dma('sp', krr.t[0:64, 0, :], MK[512:576, :], krr.sems[0], reads=[MK_d], writes=[krr.deps[0]])
                          for h in range(4):
                              with ExitStack() as st2:
                                  qn = Tile(kb, st2, "mqn", [128, NT], BF16, dma=True)
                                  qrr = Tile(kb, st2, "mqrr", [128, NT], BF16, dma=True)
                                  qrw = Tile(kb, st2, "mqrw", [128, NT], BF16, dma=True)
                                  kb.op('pool', lambda e: e.memset(qrr.ap(), 0.0), writes=[qrr.deps[0]])
                                  kb.op('pool', lambda e: e.memset(qrw.ap(), 0.0), writes=[qrw.deps[0]])
                                  kn = Tile(kb, st2, "mkn", [128, NT], BF16, dma=True)
                                  kb.dma('sp', qn.ap(), MQ[h * 256:h * 256 + 128, :], qn.sems[0], reads=[MQ_d], writes=[qn.deps[0]])
                                  kb.dma('sp', qrr.t[0:64, 0, :], MQ[h * 256 + 128:h * 256 + 192, :], qrr.sems[0], reads=[MQ_d], writes=[qrr.deps[0]])
                                  kb.dma('sp', qrw.t[0:64, 0, :], MQ[h * 256 + 192:h * 256 + 256, :], qrw.sems[0], reads=[MQ_d], writes=[qrw.deps[0]])
                                  kb.dma('sp', kn.ap(), MK[h * 128:(h + 1) * 128, :], kn.sems[0], reads=[MK_d], writes=[kn.deps[0]])
                                  V = make_V(st2, MV[h * 128:(h + 1) * 128, :], MV_d)
                                  ep = exp_post(SC_MLA)
                                  for (t0, W) in token_groups(0, NT, 512):
                                      kbs = range(0, 2) if t0 < NCTX else range(0, 34)
                                      items = []
                                      for kbk in kbs:
                                          qr_t = qrw if kbk < 2 else qrr
                                          items.append(dict(
                                              mm=[(kn.t[:, 0, kbk * 128:(kbk + 1) * 128], qn.t[:, 0, t0:t0 + W], [kn.deps[0], qn.deps[0]]),
                                                  (krr.t[:, 0, kbk * 128:(kbk + 1) * 128], qr_t.t[:, 0, t0:t0 + W], [krr.deps[0], qr_t.deps[0]])],
                                              post=lambda sb, W=W: ep(sb, W), v=V.t[:, 0, kbk, :], vd=[V.deps[0]]))
                                      attn_core(items, W)
                                      a = finalize_softmax(W, None, None, t0)
                                      g = gt.nxt()
                                      kb.dma('sp', gt.t[:, g, 0:W], pf_rows('c_g%d' % h)[:, t0:t0 + W], gt.sems[g], reads=[PF_d], writes=[gt.deps[g]])
                                      ys = yst.nxt()
                                      kb.op('pool', lambda e: e.tensor_tensor(yst.t[:, ys, 0:W], y1.t[:, a, 0:W], gt.t[:, g, 0:W], ALU.mult),
                                            reads=[y1.deps[a], gt.deps[g]], writes=[yst.deps[ys]])
                                      kb.dma('sp', YT[1024 + h * 128:1024 + (h + 1) * 128, t0:t0 + W], yst.t[:, ys, 0:W], yst.sems[ys],
                                             reads=[yst.deps[ys]], writes=[YT_d])

                if 'd' in do_mix:
                    with ExitStack() as st:
                        CW = NT + 8
                        SEGC = [(0, NCTX, 2), (NCTX, NLAT, NCTX + 6)]
                        xp = Tile(kb, st, "xp", [128, CW], F32, slots=2, dma=True)
                        ac = Tile(kb, st, "ac", [128, NT], F32, slots=2)
                        qo = Tile(kb, st, "qo", [128, NT], BF16, slots=2, dma=True)
                        for s in range(2):
                            kb.op('pool', lambda e: e.memset(xp.ap(s), 0.0), writes=[xp.deps[s]])
                        for ui in range(8):
                            nm = ('d_q%d' % ui) if ui < 4 else ('d_k%d' % (ui - 4))
                            s = xp.nxt()
                            for (tk, n, po) in SEGC:
                                kb.dma('sp', xp.t[:, s, po:po + n], pf_rows(nm)[:, tk:tk + n], xp.sems[s], reads=[PF_d], writes=[xp.deps[s]])
                            a = ac.nxt()
                            for (tk, n, po) in SEGC:
                                for j in range(4):
                                    src = xp.t[:, s, po + j - 2:po + j - 2 + n]
                                    wcol = v2.t[:, 0, 16 + ui * 4 + j:16 + ui * 4 + j + 1]
                                    if j == 0:
                                        kb.op('dve', lambda e: e.tensor_scalar(ac.t[:, a, tk:tk + n], src, wcol, None, ALU.mult),
                                              reads=[xp.deps[s], v2.deps[0]], writes=[ac.deps[a]])
                                    else:
                                        kb.op('dve', lambda e: e.scalar_tensor_tensor(ac.t[:, a, tk:tk + n], src, wcol, ac.t[:, a, tk:tk + n], ALU.mult, ALU.add),
                                              reads=[xp.deps[s], v2.deps[0], ac.deps[a]], writes=[ac.deps[a]])
                            o = qo.nxt()
                            kb.op('act', lambda e: e.activation(qo.ap(o), ac.ap(a), AF.Silu), reads=[ac.deps[a]], writes=[qo.deps[o]])
                            dst, ddep = (QD, QD_d) if ui < 4 else (KD, KD_d)
                            kb.dma('sp', dst[(ui % 4) * 128:(ui % 4 + 1) * 128, :], qo.ap(o), qo.sems[o], reads=[qo.deps[o]], writes=[ddep])
                    gcol = Tile(kb, mst, "gcol", [128, 2, 34, 4], F32)
                    with ExitStack() as st:
                        class G4:
                            def __init__(self, nm):
                                self.T = Tile(kb, st, nm, [128, NT], F32, dma=True)
                                self.t = self.T.t
                                self.deps = self.T.deps
                                self.sems = self.T.sems
                                kb.op('pool', lambda e: e.memset(self.T.ap(), 0.0), writes=[self.deps[0]])

                            def ap(self):
                                return self.T.t[0:4, 0, :]
                        A, B, C1, Dn, R = G4("gA"), G4("gB"), G4("gC"), G4("gD"), G4("gR")
                        nbv = Tile(kb, st, "nbv", [4, 4], F32)
                        onec = Tile(kb, st, "onec", [4, 1], F32)
                        kb.op('pool', lambda e: e.memset(C1.ap(), 1.0), reads=[], writes=[C1.deps[0]])
                        kb.op('pool', lambda e: e.memset(onec.ap(), 1.0), writes=[onec.deps[0]])
                        kb.op('dve', lambda e: e.tensor_scalar(nbv.ap(), v2.t[0:4, 0, 48:52], -1.0, None, ALU.mult), reads=[v2.deps[0]], writes=[nbv.deps[0]])
                        dif = pf_rows('d_if', 16, 0)

                        def rev_into(dst_t, src_t):
                            kb.op('dve', lambda e: e.tensor_copy(dst_t.t[0:4, 0, 0:NCTX], src_t.t[0:4, 0, 0:NCTX][:, ::-1]),
                                  reads=[src_t.deps[0]], writes=[dst_t.deps[0]])
                            kb.op('dve', lambda e: e.tensor_copy(dst_t.t[0:4, 0, NCTX:NT], src_t.t[0:4, 0, NCTX:NT][:, ::-1]),
                                  reads=[src_t.deps[0]], writes=[dst_t.deps[0]])
                        for dr in range(2):
                            if dr == 0:
                                kb.dma('sp', A.ap(), dif[4:8, :], A.sems[0], reads=[PF_d], writes=[A.deps[0]])
                                kb.dma('sp', B.ap(), dif[0:4, :], B.sems[0], reads=[PF_d], writes=[B.deps[0]])
                            else:
                                kb.dma('sp', R.ap(), dif[12:16, :], R.sems[0], reads=[PF_d], writes=[R.deps[0]])
                                rev_into(A, R)
                                kb.dma('sp', R.ap(), dif[8:12, :], R.sems[0], reads=[PF_d], writes=[R.deps[0]])
                                rev_into(B, R)
                            DBG_d = Dep()

                            def dbg(i, T):
                                if dr == 0 and 'DBG' in dump:
                                    kb.dma('sp', DBG[i], T.ap(), T.sems[0], reads=[T.deps[0]], writes=[DBG_d])
                                    kb.wait_all('act', [DBG_d]); kb.wait_all('dve', [DBG_d])
                            dbg(0, A)
                            kb.op('act', lambda e: e.activation(A.ap(), A.ap(), AF.Exp, bias=nbv.t[:, 0, 2 + dr:3 + dr], scale=-1.0),
                                  reads=[A.deps[0], nbv.deps[0]], writes=[A.deps[0]])
                            kb.op('act', lambda e: e.activation(A.ap(), A.ap(), AF.Ln, bias=onec.t[:, 0, :], scale=1.0),
                                  reads=[A.deps[0], onec.deps[0]], writes=[A.deps[0]])
                            dbg(1, A)
                            for c0 in range(0, NT, 256):
                                ini = 0.0 if c0 == 0 else Dn.t[0:4, 0, c0 - 1:c0]
                                kb.op('dve', lambda e: e.tensor_tensor_scan(Dn.t[0:4, 0, c0:c0 + 256], C1.t[0:4, 0, c0:c0 + 256], A.t[0:4, 0, c0:c0 + 256],
                                                                            ini, ALU.mult, ALU.add),
                                      reads=[C1.deps[0], A.deps[0], Dn.deps[0]], writes=[Dn.deps[0]])
                            dbg(2, Dn)
                            kb.op('dve', lambda e: e.scalar_tensor_tensor(B.ap(), B.ap(), v2.t[0:4, 0, 48 + dr:49 + dr], Dn.ap(), ALU.add, ALU.add),
                                  reads=[B.deps[0], v2.deps[0], Dn.deps[0]], writes=[B.deps[0]])
                            for c0 in range(0, NT, 256):
                                ini = 0.0 if c0 == 0 else A.t[0:4, 0, c0 - 1:c0]
                                kb.op('dve', lambda e: e.tensor_tensor_scan(A.t[0:4, 0, c0:c0 + 256], C1.t[0:4, 0, c0:c0 + 256], B.t[0:4, 0, c0:c0 + 256],
                                                                            ini, ALU.mult, ALU.max),
                                      reads=[C1.deps[0], B.deps[0], A.deps[0]], writes=[A.deps[0]])
                            kb.op('dve', lambda e: e.tensor_scalar(A.ap(), A.ap(), -1.0, None, ALU.mult), reads=[A.deps[0]], writes=[A.deps[0]])
                            kb.op('dve', lambda e: e.tensor_tensor(Dn.ap(), Dn.ap(), A.ap(), ALU.add), reads=[Dn.deps[0], A.deps[0]], writes=[Dn.deps[0]])
                            outs = [B, A, Dn]
                            for ki, tl in enumerate(outs):
                                srct = tl
                                if dr == 1:
                                    rev_into(R, tl)
                                    srct = R
                                kb.dma('sp', GS[dr, ki], srct.ap(), srct.sems[0], reads=[srct.deps[0]], writes=[GS_d])
                                if ki == 0:
                                    for b0 in range(0, 34, 4):
                                        nbk = min(4, 34 - b0)
                                        for j in range(nbk):
                                            kb.op('pe', lambda e: e.matmul(mb1.t[:, 0, j * 4:(j + 1) * 4], srct.t[:, 0, (b0 + j) * 128:(b0 + j + 1) * 128],
                                                                           ident_f[:, 0:4], start=True, stop=True),
                                                  reads=[srct.deps[0]] + CD, writes=[mb1.deps[0]], inc=(j == nbk - 1))
                                        kb.op('dve', lambda e: e.tensor_copy(gcol.t[:, 0, dr, b0:b0 + nbk, :],
                                                                             mb1.t[:, 0, 0:nbk * 4].rearrange("p (a b) -> p a b", a=nbk)),
                                              reads=[mb1.deps[0]], writes=[gcol.deps[0]])
                    with ExitStack() as st:
                        selT = Tile(kb, st, "selT", [128, 4, 128], F32, dma=True)
                        kb.dma('sp', selT.ap(), sel_in, selT.sems[0], writes=[selT.deps[0]])
                        msk = Tile(kb, st, "msk", [128, 8, 512], BF16, dma=True)
                        kb.dma('sp', msk.ap(), dmask, msk.sems[0], writes=[msk.deps[0]])
                        grow = Tile(kb, st, "grow", [128, 2, 512], F32, slots=2, dma=True)
                        for s in range(2):
                            kb.op('pool', lambda e: e.memset(grow.ap(s), 0.0), writes=[grow.deps[s]])
                        Dt = Tile(kb, st, "Dt", [128, 512], F32, slots=3)
                        emb = Tile(kb, st, "emb", [128, 512], F32, slots=2)
                        hd = Tile(kb, st, "hd", [128, 512], F32, slots=2)
                        hs = Tile(kb, st, "hs", [128, 512], F32, slots=2)
                        g2 = Tile(kb, st, "g2", [128, 512], F32, slots=2, dma=True)
                        for h in range(4):
                            with ExitStack() as st2:
                                qc = Tile(kb, st2, "dq", [128, NT], BF16, dma=True)
                                kc = Tile(kb, st2, "dk", [128, NT], BF16, dma=True)
                                kb.dma('sp', qc.ap(), QD[h * 128:(h + 1) * 128, :], qc.sems[0], reads=[QD_d], writes=[qc.deps[0]])
                                kb.dma('sp', kc.ap(), KD[h * 128:(h + 1) * 128, :], kc.sems[0], reads=[KD_d], writes=[kc.deps[0]])
                                V = make_V(st2, pf_rows('d_v%d' % h), PB_d)
                                for (t0, W) in token_groups(0, NT, 512):
                                    hsum = None
                                    for dr in range(2):
                                        gr = grow.nxt()
                                        kb.dma('sp', grow.t[0:4, gr, 0, 0:W], GS[dr, 1][:, t0:t0 + W], grow.sems[gr], reads=[GS_d], writes=[grow.deps[gr]])
                                        kb.dma('sp', grow.t[0:4, gr, 1, 0:W], GS[dr, 2][:, t0:t0 + W], grow.sems[gr], reads=[GS_d], writes=[grow.deps[gr]])
                                        kb.op('pe', lambda e: e.matmul(mb1.t[:, 0, 0:W], selT.t[:, 0, h, :], grow.t[:, gr, 0, 0:W], start=True, stop=True),
                                              reads=[selT.deps[0], grow.deps[gr]], writes=[mb1.deps[0]])
                                        kb.op('pe', lambda e: e.matmul(mb2.t[:, 0, 0:W], selT.t[:, 0, h, :], grow.t[:, gr, 1, 0:W], start=True, stop=True),
                                              reads=[selT.deps[0], grow.deps[gr]], writes=[mb2.deps[0]])
                                        em = emb.nxt()
                                        kb.op('act', lambda e: e.activation(emb.t[:, em, 0:W], mb2.t[:, 0, 0:W], AF.Exp),
                                              reads=[mb2.deps[0]], writes=[emb.deps[em]])
                                        last = (t0 + W - 1) // 128
                                        first = t0 // 128
                                        if dr == 0:
                                            kbs = list(range(0, last + 1))
                                        elif t0 < NCTX:
                                            kbs = list(range(first, 2))
                                        else:
                                            kbs = [0, 1] + list(range(first, 34))
                                        items = []
                                        for kbk in kbs:
                                            k0 = kbk * 128
                                            dl = k0 - t0
                                            mi = None
                                            if dr == 0 and k0 + 127 > t0:
                                                mi = dl // 128
                                            if dr == 1 and (t0 >= NCTX) == (k0 >= NCTX) and k0 < t0 + W - 1:
                                                mi = 4 + dl // 128

                                            def post(sb, kbk=kbk, mi=mi, dr=dr):
                                                d = Dt.nxt()
                                                kb.op('act', lambda e: e.activation(Dt.t[:, d, 0:W], mb1.t[:, 0, 0:W], AF.Exp,
                                                                                    bias=gcol.t[:, 0, dr, kbk, h:h + 1], scale=1.0),
                                                      reads=[mb1.deps[0], gcol.deps[0]], writes=[Dt.deps[d]])
                                                s = Et.nxt()
                                                kb.op('dve', lambda e: e.scalar_tensor_tensor(Et.t[:, s, 0:W], sb.t[:, 0, 0:W], SC_ML, Dt.t[:, d, 0:W],
                                                                                              ALU.mult, ALU.mult),
                                                      reads=[sb.deps[0], Dt.deps[d]], writes=[Et.deps[s]])
                                                if mi is not None:
                                                    kb.op('pool', lambda e: e.tensor_tensor(Et.t[:, s, 0:W], Et.t[:, s, 0:W], msk.t[:, 0, mi, 0:W], ALU.mult),
                                                          reads=[Et.deps[s], msk.deps[0]], writes=[Et.deps[s]])
                                                return Et.t[:, s, 0:W], Et.deps[s]
                                            items.append(dict(mm=[(kc.t[:, 0, k0:k0 + 128], qc.t[:, 0, t0:t0 + W], [kc.deps[0], qc.deps[0]])],
                                                              post=post, v=V.t[:, 0, kbk, :], vd=[V.deps[0]]))
                                        attn_core(items, W)
                                        r = rec.nxt()
                                        kb.op('act', lambda e: e.activation(rec.t[:, r, 0:W], denb.t[:, 0, 0:W], AF.Abs),
                                              reads=[denb.deps[0]], writes=[rec.deps[r]])
                                        kb.op('dve', lambda e: e.tensor_tensor(rec.t[:, r, 0:W], rec.t[:, r, 0:W], emb.t[:, em, 0:W], ALU.max),
                                              reads=[rec.deps[r], emb.deps[em]], writes=[rec.deps[r]])
                                        kb.op('dve', lambda e: e.reciprocal(rec.t[:, r, 0:W], rec.t[:, r, 0:W]), reads=[rec.deps[r]], writes=[rec.deps[r]])
                                        hh = hd.nxt()
                                        kb.op('dve', lambda e: e.tensor_tensor(hd.t[:, hh, 0:W], numb.t[:, 0, 0:W], rec.t[:, r, 0:W], ALU.mult),
                                              reads=[numb.deps[0], rec.deps[r]], writes=[hd.deps[hh]])
                                        if dr == 0:
                                            hsum = hh
                                    h0, h1 = hsum, hh
                                    s_ = hs.nxt()
                                    kb.op('pool', lambda e: e.tensor_tensor(hs.t[:, s_, 0:W], hd.t[:, h0, 0:W], hd.t[:, h1, 0:W], ALU.add),
                                          reads=[hd.deps[h0], hd.deps[h1]], writes=[hs.deps[s_]])
                                    a = y1.nxt()
                                    kb.op('act', lambda e: e.activation(y1.t[:, a, 0:W], hs.t[:, s_, 0:W], AF.Square), reads=[hs.deps[s_]], writes=[y1.deps[a]])
                                    bk = next_bank()
                                    kb.op('pe', lambda e: e.matmul(bk.t[:, 0, 0:W], ones_f, y1.t[:, a, 0:W], start=True, stop=True),
                                          reads=[y1.deps[a]] + CD, writes=[bk.deps[0]])
                                    r = rec.nxt()
                                    kb.op('act', lambda e: e.activation(rec.t[:, r, 0:W], bk.t[:, 0, 0:W], AF.Sqrt, bias=epsT.t[:, 0, :], scale=1.0 / 128),
                                          reads=[bk.deps[0], epsT.deps[0]], writes=[rec.deps[r]])
                                    kb.op('dve', lambda e: e.reciprocal(rec.t[:, r, 0:W], rec.t[:, r, 0:W]), reads=[rec.deps[r]], writes=[rec.deps[r]])
                                    kb.op('dve', lambda e: e.scalar_tensor_tensor(hs.t[:, s_, 0:W], hs.t[:, s_, 0:W], v2.t[:, 0, 12 + h:13 + h], rec.t[:, r, 0:W],
                                                                                  ALU.mult, ALU.mult),
                                          reads=[hs.deps[s_], v2.deps[0], rec.deps[r]], writes=[hs.deps[s_]])
                                    g = gt.nxt()
                                    kb.dma('sp', gt.t[:, g, 0:W], pf_rows('d_o%d' % h)[:, t0:t0 + W], gt.sems[g], reads=[PF_d], writes=[gt.deps[g]])
                                    gg = g2.nxt()
                                    kb.dma('sp', g2.t[:, gg, 0:W], pf_rows('d_g%d' % h)[:, t0:t0 + W], g2.sems[gg], reads=[PF_d], writes=[g2.deps[gg]])
                                    kb.op('pool', lambda e: e.tensor_tensor(hs.t[:, s_, 0:W], hs.t[:, s_, 0:W], gt.t[:, g, 0:W], ALU.mult),
                                          reads=[hs.deps[s_], gt.deps[g]], writes=[hs.deps[s_]])
                                    ys = yst.nxt()
                                    kb.op('pool', lambda e: e.tensor_tensor(yst.t[:, ys, 0:W], hs.t[:, s_, 0:W], g2.t[:, gg, 0:W], ALU.mult),
                                          reads=[hs.deps[s_], g2.deps[gg]], writes=[yst.deps[ys]])
                                    kb.dma('sp', YT[1536 + h * 128:1536 + (h + 1) * 128, t0:t0 + W], yst.t[:, ys, 0:W], yst.sems[ys],
                                           reads=[yst.deps[ys]], writes=[YT_d])

        for l in range(L):
            with ExitStack() as st:
                kb.dma('sp', vecs.t[:, 0, 0:KC], norm_gT[l], vecs.sems[0], writes=[vecs.deps[0]])
                kb.dma('sp', vecs.t[:, 0, KC:KC + 48], b_modT[l], vecs.sems[0], writes=[vecs.deps[0]])
                if l == 0:
                    kb.dma('sp', vecs.t[:, 0, KC + 48:], g_finalT, vecs.sems[0], writes=[vecs.deps[0]])
                wm = Tile(kb, st, "wm", [128, KC, 128], F32, slots=3, dma=True)
                bk = next_bank()
                wsrc = w_mod[l].rearrange("(k p) n -> p k n", p=128)
                pend = {}

                def mload(cb):
                    s = wm.nxt()
                    kb.dma('sp', wm.ap(s), wsrc[:, :, cb * 128:(cb + 1) * 128], wm.sems[s], writes=[wm.deps[s]])
                    pend[cb] = s
                mload(0)
                mload(1)
                for cb in range(48):
                    s = pend.pop(cb)
                    for k in range(KC):
                        kb.op('pe', lambda e, s=s, k=k, cb=cb: e.matmul(
                            bk.t[:, 0, cb * 2:cb * 2 + 2], wm.t[:, s, k, :], sccT.t[:, 0, k, :],
                            start=(k == 0), stop=(k == KC - 1)),
                            reads=[wm.deps[s], sccT.deps[0]], writes=[bk.deps[0]], inc=(k == KC - 1))
                    if cb + 2 < 48:
                        mload(cb + 2)
                pv = bk.t[:, 0, 0:96].rearrange("p (c t) -> p c t", t=2)
                for t in range(2):
                    kb.op('dve', lambda e, t=t: e.tensor_tensor(
                        modv.t[:, 0, :, t], pv[:, :, t], vecs.t[:, 0, KC:KC + 48], ALU.add),
                        reads=[bk.deps[0], vecs.deps[0]], writes=[modv.deps[0]])
                for t in range(2):
                    kb.op('dve', lambda e, t=t: e.scalar_tensor_tensor(
                        g1.t[:, 0, :, t], modv.t[:, 0, 16:32, t], 1.0, vecs.t[:, 0, 0:KC], ALU.add, ALU.mult),
                        reads=[modv.deps[0], vecs.deps[0]], writes=[g1.deps[0]])

            with ExitStack() as st01:
                hnT = Tile(kb, st01, "hnT", [128, KC, NT], BF16, dma=True)
                with ExitStack() as st:
                    xt = Tile(kb, st, "xt", [128, KC, 256], F32, slots=2, dma=True)
                    sq = Tile(kb, st, "sq", [128, 256], F32, slots=3)
                    rs = Tile(kb, st, "rs", [128, 256], F32, slots=2)
                    rstd = Tile(kb, st, "rstd", [128, 256], F32, slots=2)
                    tmp = Tile(kb, st, "tmp", [128, 256], F32, slots=3)
                    for (t0, w) in token_groups(0, NT, 256):
                        tc = 1 if t0 < NCTX else 0
                        s = xt.nxt()
                        kb.dma('sp', xt.t[:, s, :, 0:w], fm(XT)[:, :, t0:t0 + w], xt.sems[s],
                               reads=xtd(t0, t0 + w), writes=[xt.deps[s]])
                        bk = next_bank()
                        for c in range(KC):
                            q = sq.nxt()
                            kb.op('act', lambda e, c=c, q=q, s=s: e.activation(sq.t[:, q, 0:w], xt.t[:, s, c, 0:w], AF.Square),
                                  reads=[xt.deps[s]], writes=[sq.deps[q]])
                            kb.op('pe', lambda e, c=c, q=q: e.matmul(bk.t[:, 0, 0:w], ones_f, sq.t[:, q, 0:w],
                                                                   start=(c == 0), stop=(c == KC - 1)),
                                  reads=[sq.deps[q]] + CD, writes=[bk.deps[0]], inc=True)
                        r = rs.nxt()
                        kb.op('act', lambda e, r=r: e.activation(rs.t[:, r, 0:w], bk.t[:, 0, 0:w], AF.Sqrt,
                                                              bias=epsT.t[:, 0, :], scale=1.0 / D),
                              reads=[bk.deps[0], epsT.deps[0]], writes=[rs.deps[r]])
                        r2 = rstd.nxt()
                        kb.op('dve', lambda e, r=r, r2=r2: e.reciprocal(rstd.t[:, r2, 0:w], rs.t[:, r, 0:w]),
                              reads=[rs.deps[r]], writes=[rstd.deps[r2]])
                        for c in range(KC):
                            q = tmp.nxt()
                            kb.op('dve', lambda e, c=c, q=q, s=s, r2=r2: e.scalar_tensor_tensor(
                                tmp.t[:, q, 0:w], xt.t[:, s, c, 0:w], g1.t[:, 0, c, tc:tc + 1], rstd.t[:, r2, 0:w],
                                ALU.mult, ALU.mult),
                                reads=[xt.deps[s], g1.deps[0], rstd.deps[r2]], writes=[tmp.deps[q]])
                            kb.op('act', lambda e, c=c, q=q: e.activation(
                                hnT.t[:, 0, c, t0:t0 + w], tmp.t[:, q, 0:w], AF.Identity,
                                bias=modv.t[:, 0, c, tc:tc + 1], scale=1.0),
                                reads=[tmp.deps[q], modv.deps[0]], writes=[hnT.deps[0]])
                    for c in range(KC):
                        kb.dma('sp', HN[c * 128:(c + 1) * 128, :], hnT.t[:, 0, c, :], hnT.sems[0], reads=[hnT.deps[0]], writes=[HN_d])

                with ExitStack() as st:
                    stg_f = Tile(kb, st, "stgf", [128, 512], F32, slots=3, dma=True)
                    stg_b = Tile(kb, st, "stgb", [128, 512], BF16, slots=3, dma=True)
                    wsrc = w_in_r[l].rearrange("(k p) n -> p k n", p=128)
                    groups = token_groups(0, NT, 512)

                    def body(u, wap, wdep):
                        kind = UNITS[u][2]
                        ncol = len(UNITS[u][1])
                        for (t0, w) in groups:
                            bk = next_bank()
                            for k in range(KC):
                                kb.op('pe', lambda e, k=k: e.matmul(
                                    bk.t[0:ncol, 0, 0:w], wap[:, k, 0:ncol], hnT.t[:, 0, k, t0:t0 + w],
                                    start=(k == 0), stop=(k == KC - 1)),
                                    reads=[wdep, hnT.deps[0]], writes=[bk.deps[0]], inc=(k == KC - 1))
                            if kind == 'bf':
                                s = stg_b.nxt()
                                kb.op('dve', lambda e: e.tensor_copy(stg_b.t[0:ncol, s, 0:w], bk.t[0:ncol, 0, 0:w]),
                                      reads=[bk.deps[0]], writes=[stg_b.deps[s]])
                                kb.dma('sp', PB[u * 128:u * 128 + ncol, t0:t0 + w], stg_b.t[0:ncol, s, 0:w], stg_b.sems[s],
                                       reads=[stg_b.deps[s]], writes=[PB_d])
                            else:
                                s = stg_f.nxt()
                                if kind == 'f':
                                    kb.op('dve', lambda e: e.tensor_copy(stg_f.t[0:ncol, s, 0:w], bk.t[0:ncol, 0, 0:w]),
                                          reads=[bk.deps[0]], writes=[stg_f.deps[s]])
                                else:
                                    fn = AF.Silu if kind == 'silu' else AF.Sigmoid
                                    kb.op('act', lambda e: e.activation(stg_f.t[0:ncol, s, 0:w], bk.t[0:ncol, 0, 0:w], fn),
                                          reads=[bk.deps[0]], writes=[stg_f.deps[s]])
                                r0 = (u - NBF_U) * 128
                                kb.dma('sp', PF[r0:r0 + ncol, t0:t0 + w], stg_f.t[0:ncol, s, 0:w], stg_f.sems[s],
                                       reads=[stg_f.deps[s]], writes=[PF_d])
                    stream_units(st, NU, lambda u: wsrc[:, :, u * 128:(u + 1) * 128], KC, body)

            if stub_y:
                with ExitStack() as st:
                    yb = Tile(kb, st, "ystub", [128, KC, 512], BF16, slots=2, dma=True)
                    for (t0, w) in token_groups(0, NT, 512):
                        s = yb.nxt()
                        kb.dma('sp', yb.t[:, s, :, 0:w], fm(y_stub)[:, :, t0:t0 + w], yb.sems[s], writes=[yb.deps[s]])
                        kb.dma('sp', fm(YT)[:, :, t0:t0 + w], yb.t[:, s, :, 0:w], yb.sems[s], reads=[yb.deps[s]], writes=[YT_d])
            else:
                mixers(l)

            for hf in range(NPART):
                lo, hi = hf * PART, (hf + 1) * PART
                groups = token_groups(lo, hi, 512)
                with ExitStack() as st:
                    hnh = Tile(kb, st, "hnh", [128, KC, PART], BF16, dma=True)
                    yth = Tile(kb, st, "yth", [128, KC, PART], BF16, dma=True)
                    for c in range(KC):
                        kb.dma('sp', hnh.t[:, 0, c, :], HN[c * 128:(c + 1) * 128, lo:hi], hnh.sems[0], reads=[HN_d], writes=[hnh.deps[0]])
                        kb.dma('sp', yth.t[:, 0, c, :], YT[c * 128:(c + 1) * 128, lo:hi], yth.sems[0], reads=[YT_d], writes=[yth.deps[0]])
                    acc = Tile(kb, st, "acc", [128, PART], F32, slots=2)
                    accb = Tile(kb, st, "accb", [128, PART], BF16, slots=2, dma=True)
                    sg = Tile(kb, st, "sg", [128, 512], F32, slots=3)
                    pr = Tile(kb, st, "pr", [128, 512], F32, slots=3)
                    wbr = Tile(kb, st, "wbr", [128, 4, 128], F32, slots=3, dma=True)
                    wbrb = Tile(kb, st, "wbrb", [128, 4, 128], BF16, slots=3)
                    msrc = w_in_m[l].rearrange("(k p) n -> p k n", p=128)
                    bsrc = w_br[l].rearrange("i (k p) n -> i p k n", p=128)
                    state = {'a': 0}

                    def body(u, wap, wdep):
                        j, i = u // 4, u % 4
                        sb = wbr.nxt()
                        kb.dma('sp', wbr.ap(sb), bsrc[i][:, :, j * 128:(j + 1) * 128], wbr.sems[sb], writes=[wbr.deps[sb]])
                        sb2 = wbrb.nxt()
                        kb.op('pool', lambda e: e.tensor_copy(wbrb.ap(sb2), wbr.ap(sb)), reads=[wbr.deps[sb]], writes=[wbrb.deps[sb2]])
                        if i == 0:
                            state['a'] = acc.nxt()
                        a = state['a']
                        for (t0, w) in groups:
                            o = t0 - lo
                            bm = next_bank()
                            for k in range(KC):
                                kb.op('pe', lambda e: e.matmul(bm.t[:, 0, 0:w], wap[:, k, :], hnh.t[:, 0, k, o:o + w],
                                                               start=(k == 0), stop=(k == KC - 1)),
                                      reads=[wdep, hnh.deps[0]], writes=[bm.deps[0]], inc=(k == KC - 1))
                            bb = next_bank()
                            for k in range(4):
                                kb.op('pe', lambda e: e.matmul(bb.t[:, 0, 0:w], wbrb.t[:, sb2, k, :], yth.t[:, 0, i * 4 + k, o:o + w],
                                                               start=(k == 0), stop=(k == 3)),
                                      reads=[wbrb.deps[sb2], yth.deps[0]], writes=[bb.deps[0]], inc=(k == 3))
                            s = sg.nxt()
                            kb.op('act', lambda e: e.activation(sg.t[:, s, 0:w], bm.t[:, 0, 0:w], AF.Sigmoid),
                                  reads=[bm.deps[0]], writes=[sg.deps[s]])
                            if i == 0:
                                kb.op('dve', lambda e: e.tensor_tensor(acc.t[:, a, o:o + w], sg.t[:, s, 0:w], bb.t[:, 0, 0:w], ALU.mult),
                                      reads=[sg.deps[s], bb.deps[0]], writes=[acc.deps[a]])
                            else:
                                p = pr.nxt()
                                kb.op('dve', lambda e: e.tensor_tensor(pr.t[:, p, 0:w], sg.t[:, s, 0:w], bb.t[:, 0, 0:w], ALU.mult),
                                      reads=[sg.deps[s], bb.deps[0]], writes=[pr.deps[p]])
                                kb.op('pool', lambda e: e.tensor_tensor(acc.t[:, a, o:o + w], acc.t[:, a, o:o + w], pr.t[:, p, 0:w], ALU.add),
                                      reads=[pr.deps[p], acc.deps[a]], writes=[acc.deps[a]])
                        if i == 3:
                            ab = accb.nxt()
                            kb.op('pool', lambda e: e.tensor_copy(accb.ap(ab), acc.ap(a)), reads=[acc.deps[a]], writes=[accb.deps[ab]])
                            kb.dma('sp', ACCT[j * 128:(j + 1) * 128, lo:hi], accb.ap(ab), accb.sems[ab],
                                   reads=[accb.deps[ab]], writes=[ACCT_d])
                    stream_units(st, 64, lambda u: msrc[:, :, (u % 4) * D + (u // 4) * 128:(u % 4) * D + (u // 4) * 128 + 128], KC, body)

                with ExitStack() as st:
                    acch = Tile(kb, st, "acch", [128, KC, PART], BF16, dma=True)
                    for c in range(KC):
                        kb.dma('sp', acch.t[:, 0, c, :], ACCT[c * 128:(c + 1) * 128, lo:hi], acch.sems[0], reads=[ACCT_d], writes=[acch.deps[0]])
                    xg = Tile(kb, st, "xg", [128, PART], F32, slots=3, dma=True)
                    xn = Tile(kb, st, "xn", [128, 512], F32, slots=3, dma=True)
                    osrc = w_out[l].rearrange("(k p) n -> p k n", p=128)
                    xslot = {}

                    def pre(u):
                        s = xg.nxt()
                        kb.dma('sp', xg.ap(s), XT[u * 128:(u + 1) * 128, lo:hi], xg.sems[s],
                               reads=[XT_ds[u][hf]], writes=[xg.deps[s]])
                        xslot[u] = s

                    def body(u, wap, wdep):
                        s = xslot.pop(u)
                        for (t0, w) in groups:
                            o = t0 - lo
                            tc = 1 if t0 < NCTX else 0
                            bk = next_bank()
                            for k in range(KC):
                                kb.op('pe', lambda e: e.matmul(bk.t[:, 0, 0:w], wap[:, k, :], acch.t[:, 0, k, o:o + w],
                                                               start=(k == 0), stop=(k == KC - 1)),
                                      reads=[wdep, acch.deps[0]], writes=[bk.deps[0]], inc=(k == KC - 1))
                            s2 = xn.nxt()
                            kb.op('dve', lambda e: e.scalar_tensor_tensor(
                                xn.t[:, s2, 0:w], bk.t[:, 0, 0:w], modv.t[:, 0, 32 + u, tc:tc + 1], xg.t[:, s, o:o + w],
                                ALU.mult, ALU.add),
                                reads=[bk.deps[0], modv.deps[0], xg.deps[s]], writes=[xn.deps[s2]])
                            kb.dma('sp', XT[u * 128:(u + 1) * 128, t0:t0 + w], xn.t[:, s2, 0:w], xn.sems[s2],
                                   reads=[xn.deps[s2]], writes=[XT_ds[u][hf]])
                    stream_units(st, KC, lambda u: osrc[:, :, u * 128:(u + 1) * 128], KC, body, pre=pre)

        with ExitStack() as st:
            xt = Tile(kb, st, "fxt", [128, KC, 256], F32, slots=2, dma=True)
            sq = Tile(kb, st, "fsq", [128, 256], F32, slots=3)
            rs = Tile(kb, st, "frs", [128, 256], F32, slots=2)
            rstd = Tile(kb, st, "frstd", [128, 256], F32, slots=2)
            yn = Tile(kb, st, "fyn", [128, KC, 256], F32, slots=2)
            ot = Tile(kb, st, "fot", [128, D], F32, slots=2, dma=True)
            for (t0, w) in token_groups(NCTX, NT, 256):
                s = xt.nxt()
                kb.dma('sp', xt.ap(s), fm(XT)[:, :, t0:t0 + w], xt.sems[s], reads=xtd(t0, t0 + w), writes=[xt.deps[s]])
                bk = next_bank()
                for c in range(KC):
                    q = sq.nxt()
                    kb.op('act', lambda e, c=c, q=q, s=s: e.activation(sq.ap(q), xt.t[:, s, c, :], AF.Square),
                          reads=[xt.deps[s]], writes=[sq.deps[q]])
                    kb.op('pe', lambda e, c=c, q=q: e.matmul(bk.t[:, 0, 0:w], ones_f, sq.ap(q), start=(c == 0), stop=(c == KC - 1)),
                          reads=[sq.deps[q]] + CD, writes=[bk.deps[0]], inc=True)
                r = rs.nxt()
                kb.op('act', lambda e: e.activation(rs.ap(r), bk.t[:, 0, 0:w], AF.Sqrt, bias=epsT.t[:, 0, :], scale=1.0 / D),
                      reads=[bk.deps[0], epsT.deps[0]], writes=[rs.deps[r]])
                r2 = rstd.nxt()
                kb.op('dve', lambda e: e.reciprocal(rstd.ap(r2), rs.ap(r)), reads=[rs.deps[r]], writes=[rstd.deps[r2]])
                y = yn.nxt()
                for c in range(KC):
                    kb.op('dve', lambda e, c=c: e.scalar_tensor_tensor(
                        yn.t[:, y, c, :], xt.t[:, s, c, :], vecs.t[:, 0, KC + 48 + c:KC + 48 + c + 1], rstd.ap(r2), ALU.mult, ALU.mult),
                        reads=[xt.deps[s], vecs.deps[0], rstd.deps[r2]], writes=[yn.deps[y]])
                for tt in range(w // 128):
                    o = ot.nxt()
                    for q in range(4):
                        bk2 = next_bank()
                        for j in range(4):
                            c = q * 4 + j
                            kb.op('pe', lambda e, c=c, j=j, bk2=bk2: e.transpose(
                                bk2.t[:, 0, j * 128:(j + 1) * 128], yn.t[:, y, c, tt * 128:(tt + 1) * 128], ident_f),
                                reads=[yn.deps[y]] + CD, writes=[bk2.deps[0]], inc=(j == 3))
                        if q % 2 == 0:
                            kb.op('dve', lambda e, q=q, bk2=bk2: e.tensor_copy(ot.t[:, o, q * 512:(q + 1) * 512], bk2.t[:, 0, :]),
                                  reads=[bk2.deps[0]], writes=[ot.deps[o]])
                        else:
                            kb.op('act', lambda e, q=q, bk2=bk2: e.activation(ot.t[:, o, q * 512:(q + 1) * 512], bk2.t[:, 0, :], AF.Copy),
                                  reads=[bk2.deps[0]], writes=[ot.deps[o]])
                    r0 = t0 - NCTX + tt * 128
                    kb.dma('sp', out_d[r0:r0 + 128, :], ot.ap(o), ot.sems[o], reads=[ot.deps[o]], writes=[])
            for o in range(2):
                kb.eng['sp'].wait_ge(ot.sems[o].h, ot.sems[o].n)
        for dd in [HN_d, PB_d, PF_d, YT_d, ACCT_d] + xtd(0, NT):
            kb.wait_all('sp', [dd])
        build.last_ninst = dict(kb.ninst)
    return nc


def _na_bias_tables(rpb):
    L = rpb.shape[0]
    classes = [(10, range(-2, 3)), (0, range(0, 4)), (1, range(-1, 3)), (30, range(-2, 2)), (31, range(-3, 1))]
    out = np.zeros((L, 4, 128, 21, 128), np.float32)
    p = np.arange(128)
    rin, col = p // 64, p % 64
    idx = 0
    for (t, dl) in classes:
        for d in dl:
            kr = 2 * (t + d) + rin
            qr = 2 * t + rin
            start = np.clip(qr - 4, 0, 56)
            vrow = (kr[:, None] >= start[None, :]) & (kr[:, None] < start[None, :] + 8)
            cstart = np.clip(col - 8, 0, 48)
            vcol = (col[:, None] >= cstart[None, :]) & (col[:, None] < cstart[None, :] + 16)
            roff = np.clip(kr[:, None] - qr[None, :] + 7, 0, 14)
            coff = np.clip(col[:, None] - col[None, :] + 15, 0, 30)
            g = rpb[:, :, roff, coff]
            out[:, :, :, idx, :] = np.where((vrow & vcol)[None, None], g, np.float32(-30000.0))
            idx += 1
    return out


def host_inputs(inp, cfg=None):
    cfg = cfg or {}
    WL = cfg.get('wlayers', DEPTH)
    f = np.float32
    cols_r = np.zeros((NU * 128,), np.int64)
    valid = np.zeros((NU * 128,), bool)
    for u, (_, idx, _) in enumerate(UNITS):
        cols_r[u * 128:u * 128 + len(idx)] = idx
        valid[u * 128:u * 128 + len(idx)] = True
    w_in = np.asarray(inp['w_in'], f)[:WL]
    w_in_r = np.ascontiguousarray(np.where(valid[None, None, :], w_in[:, :, cols_r], f(0)))
    w_in_m = np.ascontiguousarray(w_in[:, :, O_MERGE:])

    def colT(v, n):
        v = np.asarray(v, f)
        return np.ascontiguousarray(v.reshape(v.shape[0], n, 128).transpose(0, 2, 1))
    consts = np.zeros((128, 3, 128), f)
    consts[:, 0, :] = np.eye(128, dtype=f)
    consts[:, 1, :] = 1.0
    consts[:, 2, :] = np.eye(128, dtype=f)[::-1]
    invc = np.zeros((4, NT), f)
    for gi, w in enumerate((2, 4, 8, 16)):
        for (t0, n) in ((0, NCTX), (NCTX, NLAT)):
            t = np.arange(n)
            lo = np.maximum(t - w // 2, 0)
            hi = np.minimum(t + (w - 1 - w // 2), n - 1)
            invc[gi, t0:t0 + n] = (1.0 / (hi - lo + 1).astype(f)).astype(f)
    invc = np.ascontiguousarray(np.broadcast_to(invc[None], (128, 4, NT)))
    vecs2 = np.zeros((WL, 128, 52), f)
    vecs2[:, :, 0:4] = colT(np.asarray(inp['pool_scale'])[:WL], 4)
    vecs2[:, :, 4:8] = colT(np.asarray(inp['mla_gq'])[:WL], 4)
    vecs2[:, :, 8:12] = colT(np.asarray(inp['mla_gkv'])[:WL], 4)
    vecs2[:, :, 12:16] = colT(np.asarray(inp['ml_gnorm'])[:WL], 4)
    cw = np.asarray(inp['conv_w'], f)[:WL]
    vecs2[:, :, 16:48] = cw.reshape(WL, 4, 8, 128).transpose(0, 3, 2, 1).reshape(WL, 128, 32)
    bif = np.asarray(inp['b_if'], f)[:WL]
    for k, (dr, ty) in enumerate(((0, 0), (1, 0), (0, 1), (1, 1))):
        vecs2[:, 0:4, 48 + k] = bif[:, dr * 8 + ty * 4:dr * 8 + ty * 4 + 4]
    wuq = np.asarray(inp['w_uq'], f)[:WL]
    perm = _rope_perm()
    cols = []
    for h in range(4):
        b = h * 192
        cols += list(b + np.arange(128)) + list(b + 128 + np.arange(64)) + list(b + 128 + perm) + list(b + 128 + np.arange(64))
    w_uq_ext = np.ascontiguousarray(wuq[:, :, np.array(cols)])
    pos = np.arange(NLAT)
    rows, colsp = (pos // 64).astype(f), (pos % 64).astype(f)
    inv = (np.float32(10000.0) ** (-np.arange(16, dtype=f) / np.float32(16))).astype(f)
    ropeT = np.zeros((64, 2, NLAT), f)
    for i in range(64):
        half, j = i // 32, i % 32
        ang = ((rows if half == 0 else colsp) * inv[j % 16]).astype(f)
        ropeT[i, 0] = np.cos(ang).astype(f)
        ropeT[i, 1] = (-np.sin(ang) if j < 16 else np.sin(ang)).astype(f)
    pp = np.arange(128)[:, None]
    ff = np.arange(512)[None, :]
    dmask = np.zeros((128, 8, 512), f)
    for k in range(4):
        dmask[:, k, :] = (ff - pp >= 128 * k)
        dmask[:, 4 + k, :] = (ff - pp <= 128 * k)
    sel = np.zeros((128, 4, 128), f)
    for h in range(4):
        sel[h, h, :] = 1.0
    shared = {
        'w_mod': np.ascontiguousarray(np.asarray(inp['w_mod'], f)[:WL]),
        'b_modT': colT(np.asarray(inp['b_mod'])[:WL], 48),
        'norm_gT': colT(np.asarray(inp['norm_g'])[:WL], KC),
        'w_in_r': w_in_r,
        'w_in_m': w_in_m,
        'w_br': np.ascontiguousarray(np.asarray(inp['w_br'], f)[:WL]),
        'w_out': np.ascontiguousarray(np.asarray(inp['w_out'], f)[:WL]),
        'g_finalT': colT(np.asarray(inp['g_final'])[None], KC)[0],
        'consts': consts,
        'invc': invc,
        'pool_w': np.ascontiguousarray(np.asarray(inp['pool_w'], f)[:WL]),
        'vecs2': vecs2,
        'w_uq_ext': w_uq_ext,
        'w_ukv': np.ascontiguousarray(np.asarray(inp['w_ukv'], f)[:WL]),
        'ropeT': ropeT,
        'na_bias': _na_bias_tables(np.asarray(inp['na_rpb'], f)[:WL]),
        'dmask': dmask.astype(ml_dtypes.bfloat16),
        'sel': sel,
    }
    maps = []
    for core in range(8):
        b = core % 4
        cc = np.stack([np.asarray(inp['c'], f)[b], np.asarray(inp['c_ctx'], f)], 0)
        ccT = np.ascontiguousarray(cc.reshape(2, KC, 128).transpose(2, 1, 0))
        m = dict(shared)
        m['x'] = np.ascontiguousarray(inp['x'][b], f)
        m['ctx'] = np.ascontiguousarray(inp['ctx'][b], f)
        m['ccT'] = ccT
        maps.append(m)
    return maps


def kernel(**inputs):
    nc = build({})
    maps = host_inputs(inputs)
    res = run_bass_kernel_spmd(nc, maps, core_ids=list(range(8)))
    return np.stack([np.asarray(res.results[b]['out'], np.float32) for b in range(4)], 0)
```

```python
import numpy as np
import ml_dtypes
from contextlib import ExitStack
import concourse.bass as bass
import concourse.mybir as mybir
from concourse.bass_utils import run_bass_kernel_spmd

F32 = mybir.dt.float32
BF16 = mybir.dt.bfloat16
AF = mybir.ActivationFunctionType
ALU = mybir.AluOpType

D = 2048
NT = 4352
NCTX = 256
NLAT = 4096
KC = 16
DEPTH = 4
EPS = 1e-6
BR = 512

O_AQKV, O_AG, O_BIN, O_BG, O_CQ, O_CKV, O_CKR, O_CG, O_DQKV, O_DO, O_DIF, O_DG, O_MERGE = (
    0, 1536, 2048, 2560, 3072, 3584, 4096, 4160, 4672, 6208, 6720, 6736, 7248)


def _rope_perm():
    p = np.arange(64)
    half = p // 32
    j = p % 32
    return half * 32 + (j + 16) % 32


def _units():
    u = []
    r = np.arange
    for h in range(4):
        u.append(('a_q%d' % h, O_AQKV + h * 128 + r(128), 'bf'))
    for h in range(4):
        u.append(('a_k%d' % h, O_AQKV + 512 + h * 128 + r(128), 'bf'))
    for h in range(4):
        u.append(('a_v%d' % h, O_AQKV + 1024 + h * 128 + r(128), 'bf'))
    for h in range(4):
        u.append(('d_v%d' % h, O_DQKV + 1024 + h * 128 + r(128), 'bf'))
    for i in range(4):
        u.append(('b_in%d' % i, O_BIN + i * 128 + r(128), 'f'))
    for i in range(4):
        u.append(('c_q%d' % i, O_CQ + i * 128 + r(128), 'f'))
    for i in range(4):
        u.append(('c_kv%d' % i, O_CKV + i * 128 + r(128), 'f'))
    u.append(('c_kr', np.concatenate([O_CKR + r(64), O_CKR + _rope_perm()]), 'f'))
    for i in range(4):
        u.append(('d_q%d' % i, O_DQKV + i * 128 + r(128), 'f'))
    for i in range(4):
        u.append(('d_k%d' % i, O_DQKV + 512 + i * 128 + r(128), 'f'))
    u.append(('d_if', O_DIF + r(16), 'f'))
    for nm, o in (('a_g', O_AG), ('b_g', O_BG), ('c_g', O_CG), ('d_g', O_DG)):
        for i in range(4):
            u.append(('%s%d' % (nm, i), o + i * 128 + r(128), 'silu'))
    for i in range(4):
        u.append(('d_o%d' % i, O_DO + i * 128 + r(128), 'sig'))
    return u


UNITS = _units()
NU = len(UNITS)
UIDX = {n: i for i, (n, _, _) in enumerate(UNITS)}
NBF_U = 16


def token_groups(lo, hi, w):
    out = []
    t = lo
    while t < hi:
        lim = NCTX if t < NCTX else hi
        ww = min(w, lim - t, hi - t)
        out.append((t, ww))
        t += ww
    return out


class Sem:
    __slots__ = ('h', 'n')

    def __init__(self, h):
        self.h = h
        self.n = 0


class Dep:
    __slots__ = ('w', 'r')

    def __init__(self):
        self.w = {}
        self.r = {}


class KB:
    def __init__(self, nc, st, n_dma_sems=40):
        self.nc = nc
        self.eng = {'pe': nc.tensor, 'act': nc.scalar, 'dve': nc.vector, 'pool': nc.gpsimd, 'sp': nc.sync}
        self.esem = {}
        for e in self.eng:
            self.esem[e] = Sem(st.enter_context(nc.semaphore('es_' + e)))
        self.seen = {e: {} for e in self.eng}
        self.free_dsems = [Sem(st.enter_context(nc.semaphore('ds%d' % i))) for i in range(n_dma_sems)]
        self.uid = 0
        self.ninst = {e: 0 for e in self.eng}

    def name(self, p):
        self.uid += 1
        return '%s_%d' % (p, self.uid)

    def dsem(self):
        return self.free_dsems.pop()

    def rel_dsem(self, s):
        self.free_dsems.append(s)

    def _waits(self, e, reads, writes):
        own = self.esem.get(e)
        need = {}
        for d in reads:
            for s, v in d.w.items():
                if v > need.get(s, 0):
                    need[s] = v
        for d in writes:
            for s, v in d.w.items():
                if s is own:
                    continue
                if v > need.get(s, 0):
                    need[s] = v
            for s, v in d.r.items():
                if s is own:
                    continue
                if v > need.get(s, 0):
                    need[s] = v
        seen = self.seen[e]
        for s, v in need.items():
            if seen.get(s, 0) >= v:
                continue
            self.eng[e].wait_ge(s.h, v)
            seen[s] = v
            self.ninst[e] += 1

    def op(self, e, fn, reads=(), writes=(), inc=True):
        self._waits(e, reads, writes)
        own = self.esem[e]
        ins = fn(self.eng[e])
        self.ninst[e] += 1
        if inc:
            own.n += 1
            ins.then_inc(own.h, 1)
            val = own.n
        else:
            val = own.n + 1
        for d in reads:
            if d.r.get(own, 0) < val:
                d.r[own] = val
        for d in writes:
            d.w[own] = val
        return ins

    def dma(self, q, out, in_, sem, reads=(), writes=()):
        self._waits(q, reads, writes)
        ins = self.eng[q].dma_start(out=out, in_=in_)
        self.ninst[q] += 1
        sem.n += 16
        ins.then_inc(sem.h, 16)
        for d in reads:
            d.r[sem] = sem.n
        for d in writes:
            d.w[sem] = sem.n
        return ins

    def wait_all(self, e, deps):
        self._waits(e, deps, deps)


class Tile:
    def __init__(self, kb, st, name, shape, dtype, slots=1, psum=False, dma=False):
        nc = kb.nc
        full = [shape[0], slots] + list(shape[1:])
        alloc = nc.psum_tensor if psum else nc.sbuf_tensor
        self.t = st.enter_context(alloc(kb.name(name), full, dtype))
        self.slots = slots
        self.deps = [Dep() for _ in range(slots)]
        self.sems = None
        self.kb = kb
        if dma:
            self.sems = [kb.dsem() for _ in range(slots)]
            st.callback(lambda: [kb.rel_dsem(s) for s in self.sems])
        self.i = -1
        st.callback(lambda: [kb.wait_all(e, self.deps) for e in kb.eng])

    def nxt(self):
        self.i = (self.i + 1) % self.slots
        return self.i

    def ap(self, s=0):
        return self.t[:, s]


def build(cfg):
    L = cfg.get('layers', DEPTH)
    dump = set(cfg.get('dump', ()))
    stub_y = cfg.get('stub_y', False)
    do_mix = cfg.get('mixers', ('a', 'b', 'c', 'd'))
    nc = bass.Bass("TRN2", target_bir_lowering=False)
    WL = cfg.get('wlayers', DEPTH)

    def dram(name, shape, dt, kind=None):
        if kind is None:
            kind = "ExternalOutput" if name in dump else "Internal"
        return nc.dram_tensor(name, list(shape), dt, kind=kind).ap()

    x_in = dram("x", [NLAT, D], F32, "ExternalInput")
    ctx_in = dram("ctx", [NCTX, D], F32, "ExternalInput")
    ccT_in = dram("ccT", [128, KC, 2], F32, "ExternalInput")
    w_mod = dram("w_mod", [WL, 48, 128, KC * 128], F32, "ExternalInput")
    b_modT = dram("b_modT", [WL, 128, 48], F32, "ExternalInput")
    norm_gT = dram("norm_gT", [WL, 128, KC], F32, "ExternalInput")
    w_in_r = dram("w_in_r", [WL, NU, 128, KC * 128], F32, "ExternalInput")
    w_in_m = dram("w_in_m", [WL, 64, 128, KC * 128], F32, "ExternalInput")
    w_br = dram("w_br", [WL, 4, 16, 128, 4 * 128], F32, "ExternalInput")
    w_out = dram("w_out", [WL, 16, 128, KC * 128], F32, "ExternalInput")
    g_finalT = dram("g_finalT", [128, KC], F32, "ExternalInput")
    consts = dram("consts", [128, 3, 128], F32, "ExternalInput")
    invc = dram("invc", [128, 4, NT], F32, "ExternalInput")
    pool_w = dram("pool_w", [WL, 4, 128, 128], F32, "ExternalInput")
    vecs2 = dram("vecs2", [WL, 128, 52], F32, "ExternalInput")
    w_uq_ext = dram("w_uq_ext", [WL, 512, 1280], F32, "ExternalInput")
    w_ukv = dram("w_ukv", [WL, 512, 1024], F32, "ExternalInput")
    ropeT = dram("ropeT", [64, 2, NLAT], F32, "ExternalInput")
    na_bias = dram("na_bias", [WL, 4, 128, 21, 128], F32, "ExternalInput")
    dmask = dram("dmask", [128, 8, 512], BF16, "ExternalInput")
    sel_in = dram("sel", [128, 4, 128], F32, "ExternalInput")
    MQ = dram("MQ", [1024, NT], BF16)
    MK = dram("MK", [576, NT], BF16)
    MV = dram("MV", [512, NT], BF16)
    QD = dram("QD", [512, NT], BF16)
    KD = dram("KD", [512, NT], BF16)
    GS = dram("GS", [2, 3, 4, NT], F32)
    DBG = dram("DBG", [4, 4, NT], F32)
    if stub_y:
        y_stub = dram("y_stub", [D, NT], BF16, "ExternalInput")
    out_d = dram("out", [NLAT, D], F32, "ExternalOutput")

    XT = dram("XT", [D, NT], F32)
    HN = dram("HN", [D, NT], BF16)
    PB = dram("PB", [NBF_U * 128, NT], BF16)
    PF = dram("PF", [(NU - NBF_U) * 128, NT], F32)
    YT = dram("YT", [D, NT], BF16)
    ACCT = dram("ACCT", [D, NT], BF16)

    def fm(ap):
        return ap.rearrange("(c p) t -> p c t", p=128)

    with ExitStack() as gst:
        kb = KB(nc, gst)
        cst = Tile(kb, gst, "cst", [128, 3, 128], F32, dma=True)
        cst_bf = Tile(kb, gst, "cstbf", [128, 3, 128], BF16)
        epsT = Tile(kb, gst, "eps", [128, 1], F32)
        banks = [Tile(kb, gst, "bank%d" % i, [128, 512], F32, psum=True) for i in range(7)]
        bankT = Tile(kb, gst, "bankT", [128, 1024], BF16, psum=True)
        modv = Tile(kb, gst, "modv", [128, 48, 2], F32)
        g1 = Tile(kb, gst, "g1", [128, KC, 2], F32)
        vecs = Tile(kb, gst, "vecs", [128, KC + 48 + KC], F32, dma=True)
        ccT = Tile(kb, gst, "ccT", [128, KC, 2], F32, dma=True)
        sccT = Tile(kb, gst, "sccT", [128, KC, 2], F32)

        kb.dma('sp', cst.ap(), consts, cst.sems[0], writes=[cst.deps[0]])
        kb.op('dve', lambda e: e.tensor_copy(cst_bf.ap(), cst.ap()), reads=[cst.deps[0]], writes=[cst_bf.deps[0]])
        kb.op('dve', lambda e: e.memset(epsT.ap(), EPS), writes=[epsT.deps[0]])
        ident_f = cst.t[:, 0, 0, :]
        ones_f = cst.t[:, 0, 1, :]
        ident_b = cst_bf.t[:, 0, 0, :]
        ones_b = cst_bf.t[:, 0, 1, :]
        CD = [cst.deps[0]]
        CBD = [cst_bf.deps[0]]
        kb.dma('sp', ccT.ap(), ccT_in, ccT.sems[0], writes=[ccT.deps[0]])
        kb.op('act', lambda e: e.activation(sccT.ap(), ccT.ap(), AF.Silu), reads=[ccT.deps[0]], writes=[sccT.deps[0]])

        bank_i = [0]

        def next_bank():
            b = banks[bank_i[0] % 7]
            bank_i[0] += 1
            return b

        NPART = 4
        PART = NT // NPART
        XT_ds = [[Dep() for _ in range(NPART)] for _ in range(KC)]

        def xtd(t0, t1, us=None):
            ps = range(t0 // PART, (t1 - 1) // PART + 1)
            return [XT_ds[u][p] for u in (range(KC) if us is None else us) for p in ps]
        HN_d = Dep()
        PB_d = Dep()
        PF_d = Dep()
        YT_d = Dep()
        ACCT_d = Dep()

        with ExitStack() as st:
            xin = Tile(kb, st, "xin", [128, D], F32, slots=2, dma=True)
            xo = Tile(kb, st, "xo", [128, KC, 128], F32, slots=2, dma=True)
            for ti in range(NT // 128):
                s = xin.nxt()
                src = ctx_in[ti * 128:(ti + 1) * 128, :] if ti < 2 else x_in[(ti - 2) * 128:(ti - 1) * 128, :]
                kb.dma('sp', xin.ap(s), src, xin.sems[s], writes=[xin.deps[s]])
                so = xo.nxt()
                for q in range(4):
                    bk = next_bank()
                    for j in range(4):
                        c = q * 4 + j
                        kb.op('pe', lambda e, c=c, j=j, bk=bk, s=s: e.transpose(
                            bk.t[:, 0, j * 128:(j + 1) * 128], xin.t[:, s, c * 128:(c + 1) * 128], ident_f),
                            reads=[xin.deps[s]] + CD, writes=[bk.deps[0]], inc=(j == 3))
                    eng = 'dve' if q % 2 == 0 else 'act'
                    if eng == 'dve':
                        kb.op('dve', lambda e, q=q, bk=bk, so=so: e.tensor_copy(
                            xo.t[:, so, q * 4:(q + 1) * 4, :], bk.t[:, 0, :].rearrange("p (a b) -> p a b", a=4)),
                            reads=[bk.deps[0]], writes=[xo.deps[so]])
                    else:
                        kb.op('act', lambda e, q=q, bk=bk, so=so: e.activation(
                            xo.t[:, so, q * 4:(q + 1) * 4, :], bk.t[:, 0, :].rearrange("p (a b) -> p a b", a=4), AF.Copy),
                            reads=[bk.deps[0]], writes=[xo.deps[so]])
                kb.dma('sp', fm(XT)[:, :, ti * 128:(ti + 1) * 128], xo.ap(so), xo.sems[so],
                       reads=[xo.deps[so]], writes=xtd(ti * 128, (ti + 1) * 128))

        def stream_units(st, n_units, src_fn, kch, body, depth=2, pre=None, slots=3):
            wst = Tile(kb, st, "wst", [128, kch, 128], F32, slots=slots, dma=True)
            wbf = Tile(kb, st, "wbf", [128, kch, 128], BF16, slots=slots)
            uslot = {}

            def load(u):
                s = wst.nxt()
                kb.dma('sp', wst.ap(s), src_fn(u), wst.sems[s], writes=[wst.deps[s]])
                s2 = wbf.nxt()
                if u % 2 == 0:
                    kb.op('pool', lambda e: e.tensor_copy(wbf.ap(s2), wst.ap(s)), reads=[wst.deps[s]], writes=[wbf.deps[s2]])
                else:
                    kb.op('act', lambda e: e.activation(wbf.ap(s2), wst.ap(s), AF.Copy), reads=[wst.deps[s]], writes=[wbf.deps[s2]])
                uslot[u] = s2
                if pre is not None:
                    pre(u)
            for u in range(min(depth, n_units)):
                load(u)
            for u in range(n_units):
                s2 = uslot.pop(u)
                body(u, wbf.t[:, s2], wbf.deps[s2])
                if u + depth < n_units:
                    load(u + depth)

        stb = banks[0:3]
        numb, denb, mb1, mb2 = banks[3], banks[4], banks[5], banks[6]
        st_i = [0]
        SC_NA = 128.0 ** -0.5
        SC_MLA = 192.0 ** -0.5
        SC_ML = 128.0 ** -0.5
        MQ_d, MK_d, MV_d, QD_d, KD_d, GS_d = Dep(), Dep(), Dep(), Dep(), Dep(), Dep()

        def pf_rows(name, n=128, r0=0):
            u = UIDX[name]
            if u < NBF_U:
                return PB[u * 128 + r0:u * 128 + r0 + n, :]
            return PF[(u - NBF_U) * 128 + r0:(u - NBF_U) * 128 + r0 + n, :]

        def make_V(st, src_rows, src_dep):
            vT = Tile(kb, st, "vT", [128, NT], BF16, dma=True)
            V = Tile(kb, st, "V", [128, 34, 128], BF16)
            kb.dma('sp', vT.ap(), src_rows, vT.sems[0], reads=[src_dep], writes=[vT.deps[0]])
            for b0 in range(0, 34, 8):
                nb = min(8, 34 - b0)
                for j in range(nb):
                    kb.op('pe', lambda e: e.transpose(bankT.t[:, 0, j * 128:(j + 1) * 128],
                                                      vT.t[:, 0, (b0 + j) * 128:(b0 + j + 1) * 128], ident_b),
                          reads=[vT.deps[0]] + CBD, writes=[bankT.deps[0]], inc=(j == nb - 1))
                kb.op('dve', lambda e: e.tensor_copy(V.t[:, 0, b0:b0 + nb, :],
                                                     bankT.t[:, 0, 0:nb * 128].rearrange("p (a b) -> p a b", a=nb)),
                      reads=[bankT.deps[0]], writes=[V.deps[0]])
            return V

        def attn_core(items, W):
            n = len(items)
            sts = [None] * n

            def issue_st(i):
                sb = stb[st_i[0] % 3]
                st_i[0] += 1
                mm = items[i]['mm']
                for q, (lt, rh, dp) in enumerate(mm):
                    kb.op('pe', lambda e: e.matmul(sb.t[:, 0, 0:W], lt, rh, start=(q == 0), stop=(q == len(mm) - 1)),
                          reads=dp, writes=[sb.deps[0]], inc=(q == len(mm) - 1))
                sts[i] = sb
            issue_st(0)
            if n > 1:
                issue_st(1)
            for i in range(n):
                E, Ed = items[i]['post'](sts[i])
                if i + 2 < n:
                    issue_st(i + 2)
                kb.op('pe', lambda e: e.matmul(numb.t[:, 0, 0:W], items[i]['v'], E, start=(i == 0), stop=(i == n - 1)),
                      reads=[Ed] + items[i]['vd'], writes=[numb.deps[0]], inc=False)
                kb.op('pe', lambda e: e.matmul(denb.t[:, 0, 0:W], ones_b, E, start=(i == 0), stop=(i == n - 1)),
                      reads=[Ed] + CBD, writes=[denb.deps[0]], inc=True)

        def mixers(l):
            with ExitStack() as mst:
                v2 = Tile(kb, mst, "v2", [128, 52], F32, dma=True)
                kb.dma('sp', v2.ap(), vecs2[l], v2.sems[0], writes=[v2.deps[0]])
                Et = Tile(kb, mst, "Et", [128, 512], BF16, slots=4)
                rec = Tile(kb, mst, "rec", [128, 512], F32, slots=2)
                y1 = Tile(kb, mst, "y1", [128, 512], F32, slots=2)
                gt = Tile(kb, mst, "gt", [128, 512], F32, slots=2, dma=True)
                yst = Tile(kb, mst, "yst", [128, 512], BF16, slots=2, dma=True)

                def exp_post(scale):
                    def post(sb, W):
                        s = Et.nxt()
                        kb.op('act', lambda e: e.activation(Et.t[:, s, 0:W], sb.t[:, 0, 0:W], AF.Exp, scale=scale),
                              reads=[sb.deps[0]], writes=[Et.deps[s]])
                        return Et.t[:, s, 0:W], Et.deps[s]
                    return post

                def finalize_softmax(W, gname, yrow0, t0, ycol=None):
                    r = rec.nxt()
                    kb.op('dve', lambda e: e.reciprocal(rec.t[:, r, 0:W], denb.t[:, 0, 0:W]),
                          reads=[denb.deps[0]], writes=[rec.deps[r]])
                    a = y1.nxt()
                    kb.op('dve', lambda e: e.tensor_tensor(y1.t[:, a, 0:W], numb.t[:, 0, 0:W], rec.t[:, r, 0:W], ALU.mult),
                          reads=[numb.deps[0], rec.deps[r]], writes=[y1.deps[a]])
                    return a

                if 'b' in do_mix:
                    with ExitStack() as st:
                        PADW = 16
                        SEG = [(0, NCTX, 0), (NCTX, NLAT, NCTX + 2 * PADW)]
                        TOTW = NT + 4 * PADW
                        up = Tile(kb, st, "up", [128, TOTW], F32, slots=2, dma=True)
                        la = Tile(kb, st, "la", [128, TOTW], F32)
                        lb = Tile(kb, st, "lb", [128, TOTW], F32)
                        ic = Tile(kb, st, "ic", [128, NT], F32, dma=True)
                        dd = Tile(kb, st, "dd", [128, NT], BF16)
                        pw = Tile(kb, st, "pw", [128, 128], F32, slots=2, dma=True)
                        pwb = Tile(kb, st, "pwb", [128, 128], BF16, slots=2)
                        for s in range(2):
                            kb.op('pool', lambda e: e.memset(up.ap(s), 0.0), writes=[up.deps[s]])
                        for gi, wdw in enumerate((2, 4, 8, 16)):
                            s = up.nxt()
                            for (tk, n, po) in SEG:
                                kb.dma('sp', up.t[:, s, po + PADW:po + PADW + n], pf_rows('b_in%d' % gi)[:, tk:tk + n], up.sems[s],
                                       reads=[PF_d], writes=[up.deps[s]])
                            kb.dma('sp', ic.ap(), invc[:, gi, :], ic.sems[0], writes=[ic.deps[0]])
                            ps = pw.nxt()
                            kb.dma('sp', pw.ap(ps), pool_w[l, gi], pw.sems[ps], writes=[pw.deps[ps]])
                            kb.op('pool', lambda e: e.tensor_copy(pwb.ap(ps), pw.ap(ps)), reads=[pw.deps[ps]], writes=[pwb.deps[ps]])
                            cur, curd = up.t[:, s], up.deps[s]
                            k = 1
                            tl = [la, lb]
                            ti = 0
                            while k < wdw:
                                dst = tl[ti % 2]
                                ti += 1
                                lo = 2 * k - 1
                                kb.op('dve', lambda e: e.tensor_tensor(dst.t[:, 0, lo:TOTW], cur[:, lo:TOTW], cur[:, lo - k:TOTW - k], ALU.add),
                                      reads=[curd], writes=[dst.deps[0]])
                                cur, curd = dst.t[:, 0], dst.deps[0]
                                k *= 2
                            oth = tl[ti % 2]
                            for (tk, n, po) in SEG:
                                sh = po + PADW + wdw // 2 - 1
                                kb.op('dve', lambda e: e.tensor_tensor(oth.t[:, 0, 0:n], cur[:, sh:sh + n], ic.t[:, 0, tk:tk + n], ALU.mult),
                                      reads=[curd, ic.deps[0]], writes=[oth.deps[0]])
                                kb.op('dve', lambda e: e.tensor_tensor(dd.t[:, 0, tk:tk + n], oth.t[:, 0, 0:n], up.t[:, s, po + PADW:po + PADW + n], ALU.subtract),
                                      reads=[oth.deps[0], up.deps[s]], writes=[dd.deps[0]])
                            for (t0, W) in token_groups(0, NT, 512):
                                bk = next_bank()
                                kb.op('pe', lambda e: e.matmul(bk.t[:, 0, 0:W], pwb.ap(ps), dd.t[:, 0, t0:t0 + W], start=True, stop=True),
                                      reads=[pwb.deps[ps], dd.deps[0]], writes=[bk.deps[0]])
                                g = gt.nxt()
                                kb.dma('sp', gt.t[:, g, 0:W], pf_rows('b_g%d' % gi)[:, t0:t0 + W], gt.sems[g], reads=[PF_d], writes=[gt.deps[g]])
                                ys = yst.nxt()
                                kb.op('dve', lambda e: e.scalar_tensor_tensor(yst.t[:, ys, 0:W], bk.t[:, 0, 0:W], v2.t[:, 0, gi:gi + 1],
                                                                              gt.t[:, g, 0:W], ALU.mult, ALU.mult),
                                      reads=[bk.deps[0], v2.deps[0], gt.deps[g]], writes=[yst.deps[ys]])
                                kb.dma('sp', YT[512 + gi * 128:512 + (gi + 1) * 128, t0:t0 + W], yst.t[:, ys, 0:W], yst.sems[ys],
                                       reads=[yst.deps[ys]], writes=[YT_d])

                if 'a' in do_mix:
                    for h in range(4):
                        with ExitStack() as st:
                            qh = Tile(kb, st, "naq", [128, NT], BF16, dma=True)
                            kh = Tile(kb, st, "nak", [128, NT], BF16, dma=True)
                            nb = Tile(kb, st, "nab", [128, 21, 128], F32, dma=True)
                            tmpb = Tile(kb, st, "natmp", [128, 128], F32, slots=3)
                            kb.dma('sp', qh.ap(), pf_rows('a_q%d' % h), qh.sems[0], reads=[PB_d], writes=[qh.deps[0]])
                            kb.dma('sp', kh.ap(), pf_rows('a_k%d' % h), kh.sems[0], reads=[PB_d], writes=[kh.deps[0]])
                            kb.dma('sp', nb.ap(), na_bias[l, h], nb.sems[0], writes=[nb.deps[0]])
                            V = make_V(st, pf_rows('a_v%d' % h), PB_d)
                            ep = exp_post(SC_NA)

                            def bias_post(idx):
                                def post(sb, W):
                                    q = tmpb.nxt()
                                    kb.op('dve', lambda e: e.scalar_tensor_tensor(tmpb.t[:, q, :], sb.t[:, 0, 0:128], SC_NA, nb.t[:, 0, idx, :],
                                                                                  ALU.mult, ALU.add),
                                          reads=[sb.deps[0], nb.deps[0]], writes=[tmpb.deps[q]])
                                    s = Et.nxt()
                                    kb.op('act', lambda e: e.activation(Et.t[:, s, 0:128], tmpb.t[:, q, :], AF.Exp),
                                          reads=[tmpb.deps[q]], writes=[Et.deps[s]])
                                    return Et.t[:, s, 0:128], Et.deps[s]
                                return post

                            def item(kbk, q0, W, post):
                                return dict(mm=[(kh.t[:, 0, kbk * 128:(kbk + 1) * 128], qh.t[:, 0, q0:q0 + W], [kh.deps[0], qh.deps[0]])],
                                            post=lambda sb: post(sb, W), v=V.t[:, 0, kbk, :], vd=[V.deps[0]])
                            attn_core([item(0, 0, 256, ep), item(1, 0, 256, ep)], 256)
                            a = finalize_softmax(256, None, None, 0)
                            g = gt.nxt()
                            kb.dma('sp', gt.t[:, g, 0:256], pf_rows('a_g%d' % h)[:, 0:256], gt.sems[g], reads=[PF_d], writes=[gt.deps[g]])
                            ys = yst.nxt()
                            kb.op('pool', lambda e: e.tensor_tensor(yst.t[:, ys, 0:256], y1.t[:, a, 0:256], gt.t[:, g, 0:256], ALU.mult),
                                  reads=[y1.deps[a], gt.deps[g]], writes=[yst.deps[ys]])
                            kb.dma('sp', YT[h * 128:(h + 1) * 128, 0:256], yst.t[:, ys, 0:256], yst.sems[ys], reads=[yst.deps[ys]], writes=[YT_d])
                            for t in range(32):
                                q0 = NCTX + t * 128
                                if t == 0:
                                    dl, base = range(0, 4), 5
                                elif t == 1:
                                    dl, base = range(-1, 3), 9
                                elif t == 30:
                                    dl, base = range(-2, 2), 13
                                elif t == 31:
                                    dl, base = range(-3, 1), 17
                                else:
                                    dl, base = range(-2, 3), 0
                                items = [item(0, q0, 128, ep), item(1, q0, 128, ep)]
                                for ii, dlt in enumerate(dl):
                                    items.append(item(2 + t + dlt, q0, 128, bias_post(base + ii)))
                                attn_core(items, 128)
                                a = finalize_softmax(128, None, None, q0)
                                if t % 4 == 0:
                                    g = gt.nxt()
                                    kb.dma('sp', gt.t[:, g, :], pf_rows('a_g%d' % h)[:, q0:q0 + 512], gt.sems[g], reads=[PF_d], writes=[gt.deps[g]])
                                    ys = yst.nxt()
                                c0 = (t % 4) * 128
                                kb.op('pool', lambda e: e.tensor_tensor(yst.t[:, ys, c0:c0 + 128], y1.t[:, a, 0:128], gt.t[:, g, c0:c0 + 128], ALU.mult),
                                      reads=[y1.deps[a], gt.deps[g]], writes=[yst.deps[ys]])
                                if t % 4 == 3:
                                    kb.dma('sp', YT[h * 128:(h + 1) * 128, q0 - 384:q0 + 128], yst.t[:, ys, :], yst.sems[ys],
                                           reads=[yst.deps[ys]], writes=[YT_d])

                if 'c' in do_mix:
                    with ExitStack() as st:
                        wq = Tile(kb, st, "wq", [128, 4, 1280], BF16)
                        wkv = Tile(kb, st, "wkv", [128, 4, 1024], BF16)
                        wstg = Tile(kb, st, "wstg", [128, 4, 640], F32, slots=2, dma=True)
                        for wi, (src, dstt) in enumerate(((w_uq_ext, wq), (w_ukv, wkv))):
                            srcv = src[l].rearrange("(k p) n -> p k n", p=128)
                            pc = 640 if wi == 0 else 512
                            for hh in range(2):
                                s = wstg.nxt()
                                kb.dma('sp', wstg.t[:, s, :, 0:pc], srcv[:, :, hh * pc:(hh + 1) * pc], wstg.sems[s], writes=[wstg.deps[s]])
                                kb.op('pool', lambda e: e.tensor_copy(dstt.t[:, 0, :, hh * pc:(hh + 1) * pc], wstg.t[:, s, :, 0:pc]),
                                      reads=[wstg.deps[s]], writes=[dstt.deps[0]])
                        cin = Tile(kb, st, "cin", [128, 4, 512], F32, slots=2, dma=True)
                        cn = Tile(kb, st, "cn", [128, 4, 512], BF16, slots=2)
                        sq = Tile(kb, st, "csq", [128, 512], F32, slots=3)
                        rs = Tile(kb, st, "crs", [128, 512], F32, slots=2)
                        rsd = Tile(kb, st, "crsd", [128, 512], F32, slots=2)
                        ctmp = Tile(kb, st, "ctmp", [128, 512], F32, slots=2)
                        krt = Tile(kb, st, "krt", [64, 2, 512], F32, slots=2, dma=True)
                        rope = Tile(kb, st, "rope", [64, 2, 512], F32, slots=2, dma=True)
                        r1 = Tile(kb, st, "r1", [64, 512], F32, slots=2)
                        r2 = Tile(kb, st, "r2", [64, 512], F32, slots=2)
                        ob = Tile(kb, st, "cob", [128, 512], BF16, slots=4, dma=True)

                        def store(rows_ap, src_fn, M, W, dep):
                            o = ob.nxt()
                            src_fn(ob.t[0:M, o, 0:W], ob.deps[o])
                            kb.dma('sp', rows_ap, ob.t[0:M, o, 0:W], ob.sems[o], reads=[ob.deps[o]], writes=[dep])

                        def proj(wt, c0, M, rhs_t, rs_, W):
                            bk = next_bank()
                            for k in range(4):
                                kb.op('pe', lambda e: e.matmul(bk.t[0:M, 0, 0:W], wt.t[:, 0, k, c0:c0 + M], rhs_t.t[:, rs_, k, 0:W],
                                                               start=(k == 0), stop=(k == 3)),
                                      reads=[wt.deps[0], rhs_t.deps[rs_]], writes=[bk.deps[0]], inc=(k == 3))
                            return bk

                        def copy_to(bk, M, W, eng='dve'):
                            def f(out_ap, odep):
                                if eng == 'dve':
                                    kb.op('dve', lambda e: e.tensor_copy(out_ap, bk.t[0:M, 0, 0:W]), reads=[bk.deps[0]], writes=[odep])
                                else:
                                    kb.op('act', lambda e: e.activation(out_ap, bk.t[0:M, 0, 0:W], AF.Copy), reads=[bk.deps[0]], writes=[odep])
                            return f

                        def rope_to(A_ap, A_dep, B_ap, B_dep, rp, W):
                            def f(out_ap, odep):
                                a = r1.nxt()
                                kb.op('dve', lambda e: e.tensor_tensor(r1.t[:, a, 0:W], A_ap, rope.t[:, rp, 0, 0:W], ALU.mult),
                                      reads=[A_dep, rope.deps[rp]], writes=[r1.deps[a]])
                                b = r2.nxt()
                                kb.op('dve', lambda e: e.tensor_tensor(r2.t[:, b, 0:W], B_ap, rope.t[:, rp, 1, 0:W], ALU.mult),
                                      reads=[B_dep, rope.deps[rp]], writes=[r2.deps[b]])
                                kb.op('pool', lambda e: e.tensor_tensor(out_ap, r1.t[:, a, 0:W], r2.t[:, b, 0:W], ALU.add),
                                      reads=[r1.deps[a], r2.deps[b]], writes=[odep])
                            return f

                        c1cut = cfg.get('c1_cut', 99)
                        for (t0, W) in (token_groups(0, NT, 512) if c1cut >= 1 else []):
                            lat = (t0 >= NCTX) and not cfg.get('no_rope', False)
                            if lat:
                                rp = rope.nxt()
                                kb.dma('sp', rope.t[:, rp, :, 0:W], ropeT[:, :, t0 - NCTX:t0 - NCTX + W], rope.sems[rp], writes=[rope.deps[rp]])
                            normed = []
                            for which, gcol in (('c_q', 4), ('c_kv', 8)):
                                ci = cin.nxt()
                                for c in range(4):
                                    kb.dma('sp', cin.t[:, ci, c, 0:W], pf_rows('%s%d' % (which, c))[:, t0:t0 + W], cin.sems[ci],
                                           reads=[PF_d], writes=[cin.deps[ci]])
                                bk = next_bank()
                                for c in range(4):
                                    q = sq.nxt()
                                    kb.op('act', lambda e: e.activation(sq.t[:, q, 0:W], cin.t[:, ci, c, 0:W], AF.Square),
                                          reads=[cin.deps[ci]], writes=[sq.deps[q]])
                                    kb.op('pe', lambda e: e.matmul(bk.t[:, 0, 0:W], ones_f, sq.t[:, q, 0:W], start=(c == 0), stop=(c == 3)),
                                          reads=[sq.deps[q]] + CD, writes=[bk.deps[0]], inc=True)
                                r = rs.nxt()
                                kb.op('act', lambda e: e.activation(rs.t[:, r, 0:W], bk.t[:, 0, 0:W], AF.Sqrt, bias=epsT.t[:, 0, :], scale=1.0 / 512),
                                      reads=[bk.deps[0], epsT.deps[0]], writes=[rs.deps[r]])
                                rd = rsd.nxt()
                                kb.op('dve', lambda e: e.reciprocal(rsd.t[:, rd, 0:W], rs.t[:, r, 0:W]), reads=[rs.deps[r]], writes=[rsd.deps[rd]])
                                ni = cn.nxt()
                                for c in range(4):
                                    kb.op('dve', lambda e: e.scalar_tensor_tensor(cn.t[:, ni, c, 0:W], cin.t[:, ci, c, 0:W], v2.t[:, 0, gcol + c:gcol + c + 1],
                                                                                  rsd.t[:, rd, 0:W], ALU.mult, ALU.mult),
                                          reads=[cin.deps[ci], v2.deps[0], rsd.deps[rd]], writes=[cn.deps[ni]])
                                normed.append(ni)
                            nq, nkv = normed
                            for h in (range(4) if c1cut >= 2 else []):
                                bk = proj(wq, h * 320, 128, cn, nq, W)
                                store(MQ[h * 256:h * 256 + 128, t0:t0 + W], copy_to(bk, 128, W, 'act'), 128, W, MQ_d)
                                if c1cut < 3:
                                    continue
                                bA = proj(wq, h * 320 + 128, 128, cn, nq, W)
                                bB = proj(wq, h * 320 + 192, 128, cn, nq, W)
                                store(MQ[h * 256 + 192:h * 256 + 256, t0:t0 + W], copy_to(bA, 64, W, 'dve'), 64, W, MQ_d)
                                if lat:
                                    store(MQ[h * 256 + 128:h * 256 + 192, t0:t0 + W],
                                          rope_to(bA.t[0:64, 0, 0:W], bA.deps[0], bB.t[0:64, 0, 0:W], bB.deps[0], rp, W), 64, W, MQ_d)
                                else:
                                    store(MQ[h * 256 + 128:h * 256 + 192, t0:t0 + W], copy_to(bA, 64, W, 'dve'), 64, W, MQ_d)
                                bk = proj(wkv, h * 256, 128, cn, nkv, W)
                                store(MK[h * 128:(h + 1) * 128, t0:t0 + W], copy_to(bk, 128, W, 'act'), 128, W, MK_d)
                                bk = proj(wkv, h * 256 + 128, 128, cn, nkv, W)
                                store(MV[h * 128:(h + 1) * 128, t0:t0 + W], copy_to(bk, 128, W, 'dve'), 128, W, MV_d)
                            if c1cut < 4:
                                continue
                            ki = krt.nxt()
                            kb.dma('sp', krt.t[:, ki, 0, 0:W], pf_rows('c_kr', 64, 0)[:, t0:t0 + W], krt.sems[ki], reads=[PF_d], writes=[krt.deps[ki]])
                            kb.dma('sp', krt.t[:, ki, 1, 0:W], pf_rows('c_kr', 64, 64)[:, t0:t0 + W], krt.sems[ki], reads=[PF_d], writes=[krt.deps[ki]])
                            if lat:
                                store(MK[512:576, t0:t0 + W], rope_to(krt.t[:, ki, 0, 0:W], krt.deps[ki], krt.t[:, ki, 1, 0:W], krt.deps[ki], rp, W), 64, W, MK_d)
                            else:
                                def cp(out_ap, odep):
                                    kb.op('dve', lambda e: e.tensor_copy(out_ap, krt.t[:, ki, 0, 0:W]), reads=[krt.deps[ki]], writes=[odep])
                                store(MK[512:576, t0:t0 + W], cp, 64, W, MK_d)
                    with ExitStack() as st:
                      if cfg.get('c_stage', 2) >= 2:
                          krr = Tile(kb, st, "krr", [128, NT], BF16, dma=True)
                          kb.op('pool', lambda e: e.memset(krr.ap(), 0.0), writes=[krr.deps[0]])
                          kb.dma('sp', krr.t[0:64, 0, :], MK[512:576, :], krr.sems[0], reads=[MK_d], writes=[krr.deps[0]])
                          for h in range(4):
                              with ExitStack() as st2:
                                  qn = Tile(kb, st2, "mqn", [128, NT], BF16, dma=True)
                                  qrr = Tile(kb, st2, "mqrr", [128, NT], BF16, dma=True)
                                  qrw = Tile(kb, st2, "mqrw", [128, NT], BF16, dma=True)
                                  kb.op('pool', lambda e: e.memset(qrr.ap(), 0.0), writes=[qrr.deps[0]])
                                  kb.op('pool', lambda e: e.memset(qrw.ap(), 0.0), writes=[qrw.deps[0]])
                                  kn = Tile(kb, st2, "mkn", [128, NT], BF16, dma=True)
                                  kb.dma('sp', qn.ap(), MQ[h * 256:h * 256 + 128, :], qn.sems[0], reads=[MQ_d], writes=[qn.deps[0]])
                                  kb.dma('sp', qrr.t[0:64, 0, :], MQ[h * 256 + 128:h * 256 + 192, :], qrr.sems[0], reads=[MQ_d], writes=[qrr.deps[0]])
                                  kb.dma('sp', qrw.t[0:64, 0, :], MQ[h * 256 + 192:h * 256 + 256, :], qrw.sems[0], reads=[MQ_d], writes=[qrw.deps[0]])
                                  kb.dma('sp', kn.ap(), MK[h * 128:(h + 1) * 128, :], kn.sems[0], reads=[MK_d], writes=[kn.deps[0]])
                                  V = make_V(st2, MV[h * 128:(h + 1) * 128, :], MV_d)
                                  ep = exp_post(SC_MLA)
                                  for (t0, W) in token_groups(0, NT, 512):
                                      kbs = range(0, 2) if t0 < NCTX else range(0, 34)
                                      items = []
                                      for kbk in kbs:
                                          qr_t = qrw if kbk < 2 else qrr
                                          items.append(dict(
                                              mm=[(kn.t[:, 0, kbk * 128:(kbk + 1) * 128], qn.t[:, 0, t0:t0 + W], [kn.deps[0], qn.deps[0]]),
                                                  (krr.t[:, 0, kbk * 128:(kbk + 1) * 128], qr_t.t[:, 0, t0:t0 + W], [krr.deps[0], qr_t.deps[0]])],
                                              post=lambda sb, W=W: ep(sb, W), v=V.t[:, 0, kbk, :], vd=[V.deps[0]]))
                                      attn_core(items, W)
                                      a = finalize_softmax(W, None, None, t0)
                                      g = gt.nxt()
                                      kb.dma('sp', gt.t[:, g, 0:W], pf_rows('c_g%d' % h)[:, t0:t0 + W], gt.sems[g], reads=[PF_d], writes=[gt.deps[g]])
                                      ys = yst.nxt()
                                      kb.op('pool', lambda e: e.tensor_tensor(yst.t[:, ys, 0:W], y1.t[:, a, 0:W], gt.t[:, g, 0:W], ALU.mult),
                                            reads=[y1.deps[a], gt.deps[g]], writes=[yst.deps[ys]])
                                      kb.dma('sp', YT[1024 + h * 128:1024 + (h + 1) * 128, t0:t0 + W], yst.t[:, ys, 0:W], yst.sems[ys],
                                             reads=[yst.deps[ys]], writes=[YT_d])

                if 'd' in do_mix:
                    with ExitStack() as st:
                        CW = NT + 8
                        SEGC = [(0, NCTX, 2), (NCTX, NLAT, NCTX + 6)]
                        xp = Tile(kb, st, "xp", [128, CW], F32, slots=2, dma=True)
                        ac = Tile(kb, st, "ac", [128, NT], F32, slots=2)
                        qo = Tile(kb, st, "qo", [128, NT], BF16, slots=2, dma=True)
                        for s in range(2):
                            kb.op('pool', lambda e: e.memset(xp.ap(s), 0.0), writes=[xp.deps[s]])
                        for ui in range(8):
                            nm = ('d_q%d' % ui) if ui < 4 else ('d_k%d' % (ui - 4))
                            s = xp.nxt()
                            for (tk, n, po) in SEGC:
                                kb.dma('sp', xp.t[:, s, po:po + n], pf_rows(nm)[:, tk:tk + n], xp.sems[s], reads=[PF_d], writes=[xp.deps[s]])
                            a = ac.nxt()
                            for (tk, n, po) in SEGC:
                                for j in range(4):
                                    src = xp.t[:, s, po + j - 2:po + j - 2 + n]
                                    wcol = v2.t[:, 0, 16 + ui * 4 + j:16 + ui * 4 + j + 1]
                                    if j == 0:
                                        kb.op('dve', lambda e: e.tensor_scalar(ac.t[:, a, tk:tk + n], src, wcol, None, ALU.mult),
                                              reads=[xp.deps[s], v2.deps[0]], writes=[ac.deps[a]])
                                    else:
                                        kb.op('dve', lambda e: e.scalar_tensor_tensor(ac.t[:, a, tk:tk + n], src, wcol, ac.t[:, a, tk:tk + n], ALU.mult, ALU.add),
                                              reads=[xp.deps[s], v2.deps[0], ac.deps[a]], writes=[ac.deps[a]])
                            o = qo.nxt()
                            kb.op('act', lambda e: e.activation(qo.ap(o), ac.ap(a), AF.Silu), reads=[ac.deps[a]], writes=[qo.deps[o]])
                            dst, ddep = (QD, QD_d) if ui < 4 else (KD, KD_d)
                            kb.dma('sp', dst[(ui % 4) * 128:(ui % 4 + 1) * 128, :], qo.ap(o), qo.sems[o], reads=[qo.deps[o]], writes=[ddep])
                    gcol = Tile(kb, mst, "gcol", [128, 2, 34, 4], F32)
                    with ExitStack() as st:
                        class G4:
                            def __init__(self, nm):
                                self.T = Tile(kb, st, nm, [128, NT], F32, dma=True)
                                self.t = self.T.t
                                self.deps = self.T.deps
                                self.sems = self.T.sems
                                kb.op('pool', lambda e: e.memset(self.T.ap(), 0.0), writes=[self.deps[0]])

                            def ap(self):
                                return self.T.t[0:4, 0, :]
                        A, B, C1, Dn, R = G4("gA"), G4("gB"), G4("gC"), G4("gD"), G4("gR")
                        nbv = Tile(kb, st, "nbv", [4, 4], F32)
                        onec = Tile(kb, st, "onec", [4, 1], F32)
                        kb.op('pool', lambda e: e.memset(C1.ap(), 1.0), reads=[], writes=[C1.deps[0]])
                        kb.op('pool', lambda e: e.memset(onec.ap(), 1.0), writes=[onec.deps[0]])
                        kb.op('dve', lambda e: e.tensor_scalar(nbv.ap(), v2.t[0:4, 0, 48:52], -1.0, None, ALU.mult), reads=[v2.deps[0]], writes=[nbv.deps[0]])
                        dif = pf_rows('d_if', 16, 0)

                        def rev_into(dst_t, src_t):
                            kb.op('dve', lambda e: e.tensor_copy(dst_t.t[0:4, 0, 0:NCTX], src_t.t[0:4, 0, 0:NCTX][:, ::-1]),
                                  reads=[src_t.deps[0]], writes=[dst_t.deps[0]])
                            kb.op('dve', lambda e: e.tensor_copy(dst_t.t[0:4, 0, NCTX:NT], src_t.t[0:4, 0, NCTX:NT][:, ::-1]),
                                  reads=[src_t.deps[0]], writes=[dst_t.deps[0]])
                        for dr in range(2):
                            if dr == 0:
                                kb.dma('sp', A.ap(), dif[4:8, :], A.sems[0], reads=[PF_d], writes=[A.deps[0]])
                                kb.dma('sp', B.ap(), dif[0:4, :], B.sems[0], reads=[PF_d], writes=[B.deps[0]])
                            else:
                                kb.dma('sp', R.ap(), dif[12:16, :], R.sems[0], reads=[PF_d], writes=[R.deps[0]])
                                rev_into(A, R)
                                kb.dma('sp', R.ap(), dif[8:12, :], R.sems[0], reads=[PF_d], writes=[R.deps[0]])
                                rev_into(B, R)
                            DBG_d = Dep()

                            def dbg(i, T):
                                if dr == 0 and 'DBG' in dump:
                                    kb.dma('sp', DBG[i], T.ap(), T.sems[0], reads=[T.deps[0]], writes=[DBG_d])
                                    kb.wait_all('act', [DBG_d]); kb.wait_all('dve', [DBG_d])
                            dbg(0, A)
                            kb.op('act', lambda e: e.activation(A.ap(), A.ap(), AF.Exp, bias=nbv.t[:, 0, 2 + dr:3 + dr], scale=-1.0),
                                  reads=[A.deps[0], nbv.deps[0]], writes=[A.deps[0]])
                            kb.op('act', lambda e: e.activation(A.ap(), A.ap(), AF.Ln, bias=onec.t[:, 0, :], scale=1.0),
                                  reads=[A.deps[0], onec.deps[0]], writes=[A.deps[0]])
                            dbg(1, A)
                            for c0 in range(0, NT, 256):
                                ini = 0.0 if c0 == 0 else Dn.t[0:4, 0, c0 - 1:c0]
                                kb.op('dve', lambda e: e.tensor_tensor_scan(Dn.t[0:4, 0, c0:c0 + 256], C1.t[0:4, 0, c0:c0 + 256], A.t[0:4, 0, c0:c0 + 256],
                                                                            ini, ALU.mult, ALU.add),
                                      reads=[C1.deps[0], A.deps[0], Dn.deps[0]], writes=[Dn.deps[0]])
                            dbg(2, Dn)
                            kb.op('dve', lambda e: e.scalar_tensor_tensor(B.ap(), B.ap(), v2.t[0:4, 0, 48 + dr:49 + dr], Dn.ap(), ALU.add, ALU.add),
                                  reads=[B.deps[0], v2.deps[0], Dn.deps[0]], writes=[B.deps[0]])
                            for c0 in range(0, NT, 256):
                                ini = 0.0 if c0 == 0 else A.t[0:4, 0, c0 - 1:c0]
                                kb.op('dve', lambda e: e.tensor_tensor_scan(A.t[0:4, 0, c0:c0 + 256], C1.t[0:4, 0, c0:c0 + 256], B.t[0:4, 0, c0:c0 + 256],
                                                                            ini, ALU.mult, ALU.max),
                                      reads=[C1.deps[0], B.deps[0], A.deps[0]], writes=[A.deps[0]])
                            kb.op('dve', lambda e: e.tensor_scalar(A.ap(), A.ap(), -1.0, None, ALU.mult), reads=[A.deps[0]], writes=[A.deps[0]])
                            kb.op('dve', lambda e: e.tensor_tensor(Dn.ap(), Dn.ap(), A.ap(), ALU.add), reads=[Dn.deps[0], A.deps[0]], writes=[Dn.deps[0]])
                            outs = [B, A, Dn]
                            for ki, tl in enumerate(outs):
                                srct = tl
                                if dr == 1:
                                    rev_into(R, tl)
                                    srct = R
                                kb.dma('sp', GS[dr, ki], srct.ap(), srct.sems[0], reads=[srct.deps[0]], writes=[GS_d])
                                if ki == 0:
                                    for b0 in range(0, 34, 4):
                                        nbk = min(4, 34 - b0)
                                        for j in range(nbk):
                                            kb.op('pe', lambda e: e.matmul(mb1.t[:, 0, j * 4:(j + 1) * 4], srct.t[:, 0, (b0 + j) * 128:(b0 + j + 1) * 128],
                                                                           ident_f[:, 0:4], start=True, stop=True),
                                                  reads=[srct.deps[0]] + CD, writes=[mb1.deps[0]], inc=(j == nbk - 1))
                                        kb.op('dve', lambda e: e.tensor_copy(gcol.t[:, 0, dr, b0:b0 + nbk, :],
                                                                             mb1.t[:, 0, 0:nbk * 4].rearrange("p (a b) -> p a b", a=nbk)),
                                              reads=[mb1.deps[0]], writes=[gcol.deps[0]])
                    with ExitStack() as st:
                        selT = Tile(kb, st, "selT", [128, 4, 128], F32, dma=True)
                        kb.dma('sp', selT.ap(), sel_in, selT.sems[0], writes=[selT.deps[0]])
                        msk = Tile(kb, st, "msk", [128, 8, 512], BF16, dma=True)
                        kb.dma('sp', msk.ap(), dmask, msk.sems[0], writes=[msk.deps[0]])
                        grow = Tile(kb, st, "grow", [128, 2, 512], F32, slots=2, dma=True)
                        for s in range(2):
                            kb.op('pool', lambda e: e.memset(grow.ap(s), 0.0), writes=[grow.deps[s]])
                        Dt = Tile(kb, st, "Dt", [128, 512], F32, slots=3)
                        emb = Tile(kb, st, "emb", [128, 512], F32, slots=2)
                        hd = Tile(kb, st, "hd", [128, 512], F32, slots=2)
                        hs = Tile(kb, st, "hs", [128, 512], F32, slots=2)
                        g2 = Tile(kb, st, "g2", [128, 512], F32, slots=2, dma=True)
                        for h in range(4):
                            with ExitStack() as st2:
                                qc = Tile(kb, st2, "dq", [128, NT], BF16, dma=True)
                                kc = Tile(kb, st2, "dk", [128, NT], BF16, dma=True)
                                kb.dma('sp', qc.ap(), QD[h * 128:(h + 1) * 128, :], qc.sems[0], reads=[QD_d], writes=[qc.deps[0]])
                                kb.dma('sp', kc.ap(), KD[h * 128:(h + 1) * 128, :], kc.sems[0], reads=[KD_d], writes=[kc.deps[0]])
                                V = make_V(st2, pf_rows('d_v%d' % h), PB_d)
                                for (t0, W) in token_groups(0, NT, 512):
                                    hsum = None
                                    for dr in range(2):
                                        gr = grow.nxt()
                                        kb.dma('sp', grow.t[0:4, gr, 0, 0:W], GS[dr, 1][:, t0:t0 + W], grow.sems[gr], reads=[GS_d], writes=[grow.deps[gr]])
                                        kb.dma('sp', grow.t[0:4, gr, 1, 0:W], GS[dr, 2][:, t0:t0 + W], grow.sems[gr], reads=[GS_d], writes=[grow.deps[gr]])
                                        kb.op('pe', lambda e: e.matmul(mb1.t[:, 0, 0:W], selT.t[:, 0, h, :], grow.t[:, gr, 0, 0:W], start=True, stop=True),
                                              reads=[selT.deps[0], grow.deps[gr]], writes=[mb1.deps[0]])
                                        kb.op('pe', lambda e: e.matmul(mb2.t[:, 0, 0:W], selT.t[:, 0, h, :], grow.t[:, gr, 1, 0:W], start=True, stop=True),
                                              reads=[selT.deps[0], grow.deps[gr]], writes=[mb2.deps[0]])
                                        em = emb.nxt()
                                        kb.op('act', lambda e: e.activation(emb.t[:, em, 0:W], mb2.t[:, 0, 0:W], AF.Exp),
                                              reads=[mb2.deps[0]], writes=[emb.deps[em]])
                                        last = (t0 + W - 1) // 128
                                        first = t0 // 128
                                        if dr == 0:
                                            kbs = list(range(0, last + 1))
                                        elif t0 < NCTX:
                                            kbs = list(range(first, 2))
                                        else:
                                            kbs = [0, 1] + list(range(first, 34))
                                        items = []
                                        for kbk in kbs:
                                            k0 = kbk * 128
                                            dl = k0 - t0
                                            mi = None
                                            if dr == 0 and k0 + 127 > t0:
                                                mi = dl // 128
                                            if dr == 1 and (t0 >= NCTX) == (k0 >= NCTX) and k0 < t0 + W - 1:
                                                mi = 4 + dl // 128

                                            def post(sb, kbk=kbk, mi=mi, dr=dr):
                                                d = Dt.nxt()
                                                kb.op('act', lambda e: e.activation(Dt.t[:, d, 0:W], mb1.t[:, 0, 0:W], AF.Exp,
                                                                                    bias=gcol.t[:, 0, dr, kbk, h:h + 1], scale=1.0),
                                                      reads=[mb1.deps[0], gcol.deps[0]], writes=[Dt.deps[d]])
                                                s = Et.nxt()
                                                kb.op('dve', lambda e: e.scalar_tensor_tensor(Et.t[:, s, 0:W], sb.t[:, 0, 0:W], SC_ML, Dt.t[:, d, 0:W],
                                                                                              ALU.mult, ALU.mult),
                                                      reads=[sb.deps[0], Dt.deps[d]], writes=[Et.deps[s]])
                                                if mi is not None:
                                                    kb.op('pool', lambda e: e.tensor_tensor(Et.t[:, s, 0:W], Et.t[:, s, 0:W], msk.t[:, 0, mi, 0:W], ALU.mult),
                                                          reads=[Et.deps[s], msk.deps[0]], writes=[Et.deps[s]])
                                                return Et.t[:, s, 0:W], Et.deps[s]
                                            items.append(dict(mm=[(kc.t[:, 0, k0:k0 + 128], qc.t[:, 0, t0:t0 + W], [kc.deps[0], qc.deps[0]])],
                                                              post=post, v=V.t[:, 0, kbk, :], vd=[V.deps[0]]))
                                        attn_core(items, W)
                                        r = rec.nxt()
                                        kb.op('act', lambda e: e.activation(rec.t[:, r, 0:W], denb.t[:, 0, 0:W], AF.Abs),
                                              reads=[denb.deps[0]], writes=[rec.deps[r]])
                                        kb.op('dve', lambda e: e.tensor_tensor(rec.t[:, r, 0:W], rec.t[:, r, 0:W], emb.t[:, em, 0:W], ALU.max),
                                              reads=[rec.deps[r], emb.deps[em]], writes=[rec.deps[r]])
                                        kb.op('dve', lambda e: e.reciprocal(rec.t[:, r, 0:W], rec.t[:, r, 0:W]), reads=[rec.deps[r]], writes=[rec.deps[r]])
                                        hh = hd.nxt()
                                        kb.op('dve', lambda e: e.tensor_tensor(hd.t[:, hh, 0:W], numb.t[:, 0, 0:W], rec.t[:, r, 0:W], ALU.mult),
                                              reads=[numb.deps[0], rec.deps[r]], writes=[hd.deps[hh]])
                                        if dr == 0:
                                            hsum = hh
                                    h0, h1 = hsum, hh
                                    s_ = hs.nxt()
                                    kb.op('pool', lambda e: e.tensor_tensor(hs.t[:, s_, 0:W], hd.t[:, h0, 0:W], hd.t[:, h1, 0:W], ALU.add),
                                          reads=[hd.deps[h0], hd.deps[h1]], writes=[hs.deps[s_]])
                                    a = y1.nxt()
                                    kb.op('act', lambda e: e.activation(y1.t[:, a, 0:W], hs.t[:, s_, 0:W], AF.Square), reads=[hs.deps[s_]], writes=[y1.deps[a]])
                                    bk = next_bank()
                                    kb.op('pe', lambda e: e.matmul(bk.t[:, 0, 0:W], ones_f, y1.t[:, a, 0:W], start=True, stop=True),
                                          reads=[y1.deps[a]] + CD, writes=[bk.deps[0]])
                                    r = rec.nxt()
                                    kb.op('act', lambda e: e.activation(rec.t[:, r, 0:W], bk.t[:, 0, 0:W], AF.Sqrt, bias=epsT.t[:, 0, :], scale=1.0 / 128),
                                          reads=[bk.deps[0], epsT.deps[0]], writes=[rec.deps[r]])
                                    kb.op('dve', lambda e: e.reciprocal(rec.t[:, r, 0:W], rec.t[:, r, 0:W]), reads=[rec.deps[r]], writes=[rec.deps[r]])
                                    kb.op('dve', lambda e: e.scalar_tensor_tensor(hs.t[:, s_, 0:W], hs.t[:, s_, 0:W], v2.t[:, 0, 12 + h:13 + h], rec.t[:, r, 0:W],
                                                                                  ALU.mult, ALU.mult),
                                          reads=[hs.deps[s_], v2.deps[0], rec.deps[r]], writes=[hs.deps[s_]])
                                    g = gt.nxt()
                                    kb.dma('sp', gt.t[:, g, 0:W], pf_rows('d_o%d' % h)[:, t0:t0 + W], gt.sems[g], reads=[PF_d], writes=[gt.deps[g]])
                                    gg = g2.nxt()
                                    kb.dma('sp', g2.t[:, gg, 0:W], pf_rows('d_g%d' % h)[:, t0:t0 + W], g2.sems[gg], reads=[PF_d], writes=[g2.deps[gg]])
                                    kb.op('pool', lambda e: e.tensor_tensor(hs.t[:, s_, 0:W], hs.t[:, s_, 0:W], gt.t[:, g, 0:W], ALU.mult),
                                          reads=[hs.deps[s_], gt.deps[g]], writes=[hs.deps[s_]])
                                    ys = yst.nxt()
                                    kb.op('pool', lambda e: e.tensor_tensor(yst.t[:, ys, 0:W], hs.t[:, s_, 0:W], g2.t[:, gg, 0:W], ALU.mult),
                                          reads=[hs.deps[s_], g2.deps[gg]], writes=[yst.deps[ys]])
                                    kb.dma('sp', YT[1536 + h * 128:1536 + (h + 1) * 128, t0:t0 + W], yst.t[:, ys, 0:W], yst.sems[ys],
                                           reads=[yst.deps[ys]], writes=[YT_d])

        for l in range(L):
            with ExitStack() as st:
                kb.dma('sp', vecs.t[:, 0, 0:KC], norm_gT[l], vecs.sems[0], writes=[vecs.deps[0]])
                kb.dma('sp', vecs.t[:, 0, KC:KC + 48], b_modT[l], vecs.sems[0], writes=[vecs.deps[0]])
                if l == 0:
                    kb.dma('sp', vecs.t[:, 0, KC + 48:], g_finalT, vecs.sems[0], writes=[vecs.deps[0]])
                wm = Tile(kb, st, "wm", [128, KC, 128], F32, slots=3, dma=True)
                bk = next_bank()
                pend = {}

                def mload(cb):
                    s = wm.nxt()
                    kb.dma('sp', wm.ap(s), w_mod[l, cb].rearrange("p (k c) -> p k c", k=KC), wm.sems[s], writes=[wm.deps[s]])
                    pend[cb] = s
                mload(0)
                mload(1)
                for cb in range(48):
                    s = pend.pop(cb)
                    for k in range(KC):
                        kb.op('pe', lambda e, s=s, k=k, cb=cb: e.matmul(
                            bk.t[:, 0, cb * 2:cb * 2 + 2], wm.t[:, s, k, :], sccT.t[:, 0, k, :],
                            start=(k == 0), stop=(k == KC - 1)),
                            reads=[wm.deps[s], sccT.deps[0]], writes=[bk.deps[0]], inc=(k == KC - 1))
                    if cb + 2 < 48:
                        mload(cb + 2)
                pv = bk.t[:, 0, 0:96].rearrange("p (c t) -> p c t", t=2)
                for t in range(2):
                    kb.op('dve', lambda e, t=t: e.tensor_tensor(
                        modv.t[:, 0, :, t], pv[:, :, t], vecs.t[:, 0, KC:KC + 48], ALU.add),
                        reads=[bk.deps[0], vecs.deps[0]], writes=[modv.deps[0]])
                for t in range(2):
                    kb.op('dve', lambda e, t=t: e.scalar_tensor_tensor(
                        g1.t[:, 0, :, t], modv.t[:, 0, 16:32, t], 1.0, vecs.t[:, 0, 0:KC], ALU.add, ALU.mult),
                        reads=[modv.deps[0], vecs.deps[0]], writes=[g1.deps[0]])

            with ExitStack() as st01:
                hnT = Tile(kb, st01, "hnT", [128, KC, NT], BF16, dma=True)
                with ExitStack() as st:
                    xt = Tile(kb, st, "xt", [128, KC, 256], F32, slots=2, dma=True)
                    sq = Tile(kb, st, "sq", [128, 256], F32, slots=3)
                    rs = Tile(kb, st, "rs", [128, 256], F32, slots=2)
                    rstd = Tile(kb, st, "rstd", [128, 256], F32, slots=2)
                    tmp = Tile(kb, st, "tmp", [128, 256], F32, slots=3)
                    for (t0, w) in token_groups(0, NT, 256):
                        tc = 1 if t0 < NCTX else 0
                        s = xt.nxt()
                        kb.dma('sp', xt.t[:, s, :, 0:w], fm(XT)[:, :, t0:t0 + w], xt.sems[s],
                               reads=xtd(t0, t0 + w), writes=[xt.deps[s]])
                        bk = next_bank()
                        for c in range(KC):
                            q = sq.nxt()
                            kb.op('act', lambda e, c=c, q=q, s=s: e.activation(sq.t[:, q, 0:w], xt.t[:, s, c, 0:w], AF.Square),
                                  reads=[xt.deps[s]], writes=[sq.deps[q]])
                            kb.op('pe', lambda e, c=c, q=q: e.matmul(bk.t[:, 0, 0:w], ones_f, sq.t[:, q, 0:w],
                                                                   start=(c == 0), stop=(c == KC - 1)),
                                  reads=[sq.deps[q]] + CD, writes=[bk.deps[0]], inc=True)
                        r = rs.nxt()
                        kb.op('act', lambda e, r=r: e.activation(rs.t[:, r, 0:w], bk.t[:, 0, 0:w], AF.Sqrt,
                                                              bias=epsT.t[:, 0, :], scale=1.0 / D),
                              reads=[bk.deps[0], epsT.deps[0]], writes=[rs.deps[r]])
                        r2 = rstd.nxt()
                        kb.op('dve', lambda e, r=r, r2=r2: e.reciprocal(rstd.t[:, r2, 0:w], rs.t[:, r, 0:w]),
                              reads=[rs.deps[r]], writes=[rstd.deps[r2]])
                        for c in range(KC):
                            q = tmp.nxt()
                            kb.op('dve', lambda e, c=c, q=q, s=s, r2=r2: e.scalar_tensor_tensor(
                                tmp.t[:, q, 0:w], xt.t[:, s, c, 0:w], g1.t[:, 0, c, tc:tc + 1], rstd.t[:, r2, 0:w],
                                ALU.mult, ALU.mult),
                                reads=[xt.deps[s], g1.deps[0], rstd.deps[r2]], writes=[tmp.deps[q]])
                            kb.op('act', lambda e, c=c, q=q: e.activation(
                                hnT.t[:, 0, c, t0:t0 + w], tmp.t[:, q, 0:w], AF.Identity,
                                bias=modv.t[:, 0, c, tc:tc + 1], scale=1.0),
                                reads=[tmp.deps[q], modv.deps[0]], writes=[hnT.deps[0]])
                    for c in range(KC):
                        kb.dma('sp', HN[c * 128:(c + 1) * 128, :], hnT.t[:, 0, c, :], hnT.sems[0], reads=[hnT.deps[0]], writes=[HN_d])

                with ExitStack() as st:
                    stg_f = Tile(kb, st, "stgf", [128, 512], F32, slots=3, dma=True)
                    stg_b = Tile(kb, st, "stgb", [128, 512], BF16, slots=3, dma=True)
                    groups = token_groups(0, NT, 512)

                    def body(u, wap, wdep):
                        kind = UNITS[u][2]
                        ncol = len(UNITS[u][1])
                        for (t0, w) in groups:
                            bk = next_bank()
                            for k in range(KC):
                                kb.op('pe', lambda e, k=k: e.matmul(
                                    bk.t[0:ncol, 0, 0:w], wap[:, k, 0:ncol], hnT.t[:, 0, k, t0:t0 + w],
                                    start=(k == 0), stop=(k == KC - 1)),
                                    reads=[wdep, hnT.deps[0]], writes=[bk.deps[0]], inc=(k == KC - 1))
                            if kind == 'bf':
                                s = stg_b.nxt()
                                kb.op('dve', lambda e: e.tensor_copy(stg_b.t[0:ncol, s, 0:w], bk.t[0:ncol, 0, 0:w]),
                                      reads=[bk.deps[0]], writes=[stg_b.deps[s]])
                                kb.dma('sp', PB[u * 128:u * 128 + ncol, t0:t0 + w], stg_b.t[0:ncol, s, 0:w], stg_b.sems[s],
                                       reads=[stg_b.deps[s]], writes=[PB_d])
                            else:
                                s = stg_f.nxt()
                                if kind == 'f':
                                    kb.op('dve', lambda e: e.tensor_copy(stg_f.t[0:ncol, s, 0:w], bk.t[0:ncol, 0, 0:w]),
                                          reads=[bk.deps[0]], writes=[stg_f.deps[s]])
                                else:
                                    fn = AF.Silu if kind == 'silu' else AF.Sigmoid
                                    kb.op('act', lambda e: e.activation(stg_f.t[0:ncol, s, 0:w], bk.t[0:ncol, 0, 0:w], fn),
                                          reads=[bk.deps[0]], writes=[stg_f.deps[s]])
                                r0 = (u - NBF_U) * 128
                                kb.dma('sp', PF[r0:r0 + ncol, t0:t0 + w], stg_f.t[0:ncol, s, 0:w], stg_f.sems[s],
                                       reads=[stg_f.deps[s]], writes=[PF_d])
                    stream_units(st, NU, lambda u: w_in_r[l, u].rearrange("p (k c) -> p k c", k=KC), KC, body)

            if stub_y:
                with ExitStack() as st:
                    yb = Tile(kb, st, "ystub", [128, KC, 512], BF16, slots=2, dma=True)
                    for (t0, w) in token_groups(0, NT, 512):
                        s = yb.nxt()
                        kb.dma('sp', yb.t[:, s, :, 0:w], fm(y_stub)[:, :, t0:t0 + w], yb.sems[s], writes=[yb.deps[s]])
                        kb.dma('sp', fm(YT)[:, :, t0:t0 + w], yb.t[:, s, :, 0:w], yb.sems[s], reads=[yb.deps[s]], writes=[YT_d])
            else:
                mixers(l)

            for hf in range(NPART):
                lo, hi = hf * PART, (hf + 1) * PART
                groups = token_groups(lo, hi, 512)
                with ExitStack() as st:
                    hnh = Tile(kb, st, "hnh", [128, KC, PART], BF16, dma=True)
                    yth = Tile(kb, st, "yth", [128, KC, PART], BF16, dma=True)
                    for c in range(KC):
                        kb.dma('sp', hnh.t[:, 0, c, :], HN[c * 128:(c + 1) * 128, lo:hi], hnh.sems[0], reads=[HN_d], writes=[hnh.deps[0]])
                        kb.dma('sp', yth.t[:, 0, c, :], YT[c * 128:(c + 1) * 128, lo:hi], yth.sems[0], reads=[YT_d], writes=[yth.deps[0]])
                    acc = Tile(kb, st, "acc", [128, PART], F32, slots=2)
                    accb = Tile(kb, st, "accb", [128, PART], BF16, slots=2, dma=True)
                    sg = Tile(kb, st, "sg", [128, 512], F32, slots=3)
                    pr = Tile(kb, st, "pr", [128, 512], F32, slots=3)
                    wbr = Tile(kb, st, "wbr", [128, 4, 128], F32, slots=4, dma=True)
                    wbrb = Tile(kb, st, "wbrb", [128, 4, 128], BF16, slots=5)
                    bslot = {}

                    def pre_a(u):
                        j, i = u // 4, u % 4
                        sb = wbr.nxt()
                        kb.dma('sp', wbr.ap(sb), w_br[l, i, j].rearrange("p (k c) -> p k c", k=4), wbr.sems[sb], writes=[wbr.deps[sb]])
                        sb2 = wbrb.nxt()
                        kb.op('dve', lambda e: e.tensor_copy(wbrb.ap(sb2), wbr.ap(sb)), reads=[wbr.deps[sb]], writes=[wbrb.deps[sb2]])
                        bslot[u] = sb2
                    state = {'a': 0}

                    def body(u, wap, wdep):
                        j, i = u // 4, u % 4
                        sb2 = bslot.pop(u)
                        if i == 0:
                            state['a'] = acc.nxt()
                        a = state['a']
                        for (t0, w) in groups:
                            o = t0 - lo
                            bm = next_bank()
                            for k in range(KC):
                                kb.op('pe', lambda e: e.matmul(bm.t[:, 0, 0:w], wap[:, k, :], hnh.t[:, 0, k, o:o + w],
                                                               start=(k == 0), stop=(k == KC - 1)),
                                      reads=[wdep, hnh.deps[0]], writes=[bm.deps[0]], inc=(k == KC - 1))
                            bb = next_bank()
                            for k in range(4):
                                kb.op('pe', lambda e: e.matmul(bb.t[:, 0, 0:w], wbrb.t[:, sb2, k, :], yth.t[:, 0, i * 4 + k, o:o + w],
                                                               start=(k == 0), stop=(k == 3)),
                                      reads=[wbrb.deps[sb2], yth.deps[0]], writes=[bb.deps[0]], inc=(k == 3))
                            s = sg.nxt()
                            kb.op('act', lambda e: e.activation(sg.t[:, s, 0:w], bm.t[:, 0, 0:w], AF.Sigmoid),
                                  reads=[bm.deps[0]], writes=[sg.deps[s]])
                            if i == 0:
                                kb.op('dve', lambda e: e.tensor_tensor(acc.t[:, a, o:o + w], sg.t[:, s, 0:w], bb.t[:, 0, 0:w], ALU.mult),
                                      reads=[sg.deps[s], bb.deps[0]], writes=[acc.deps[a]])
                            else:
                                p = pr.nxt()
                                kb.op('dve', lambda e: e.tensor_tensor(pr.t[:, p, 0:w], sg.t[:, s, 0:w], bb.t[:, 0, 0:w], ALU.mult),
                                      reads=[sg.deps[s], bb.deps[0]], writes=[pr.deps[p]])
                                kb.op('dve', lambda e: e.tensor_tensor(acc.t[:, a, o:o + w], acc.t[:, a, o:o + w], pr.t[:, p, 0:w], ALU.add),
                                      reads=[pr.deps[p], acc.deps[a]], writes=[acc.deps[a]])
                        if i == 3:
                            ab = accb.nxt()
                            kb.op('pool', lambda e: e.tensor_copy(accb.ap(ab), acc.ap(a)), reads=[acc.deps[a]], writes=[accb.deps[ab]])
                            kb.dma('sp', ACCT[j * 128:(j + 1) * 128, lo:hi], accb.ap(ab), accb.sems[ab],
                                   reads=[accb.deps[ab]], writes=[ACCT_d])
                    stream_units(st, 64, lambda u: w_in_m[l, (u % 4) * 16 + (u // 4)].rearrange("p (k c) -> p k c", k=KC), KC, body,
                                 depth=3, pre=pre_a, slots=4)

                with ExitStack() as st:
                    acch = Tile(kb, st, "acch", [128, KC, PART], BF16, dma=True)
                    for c in range(KC):
                        kb.dma('sp', acch.t[:, 0, c, :], ACCT[c * 128:(c + 1) * 128, lo:hi], acch.sems[0], reads=[ACCT_d], writes=[acch.deps[0]])
                    xg = Tile(kb, st, "xg", [128, PART], F32, slots=4, dma=True)
                    xn = Tile(kb, st, "xn", [128, 512], F32, slots=3, dma=True)
                    xslot = {}

                    def pre(u):
                        s = xg.nxt()
                        kb.dma('sp', xg.ap(s), XT[u * 128:(u + 1) * 128, lo:hi], xg.sems[s],
                               reads=[XT_ds[u][hf]], writes=[xg.deps[s]])
                        xslot[u] = s

                    def body(u, wap, wdep):
                        s = xslot.pop(u)
                        for (t0, w) in groups:
                            o = t0 - lo
                            tc = 1 if t0 < NCTX else 0
                            bk = next_bank()
                            for k in range(KC):
                                kb.op('pe', lambda e: e.matmul(bk.t[:, 0, 0:w], wap[:, k, :], acch.t[:, 0, k, o:o + w],
                                                               start=(k == 0), stop=(k == KC - 1)),
                                      reads=[wdep, acch.deps[0]], writes=[bk.deps[0]], inc=(k == KC - 1))
                            s2 = xn.nxt()
                            kb.op('dve', lambda e: e.scalar_tensor_tensor(
                                xn.t[:, s2, 0:w], bk.t[:, 0, 0:w], modv.t[:, 0, 32 + u, tc:tc + 1], xg.t[:, s, o:o + w],
                                ALU.mult, ALU.add),
                                reads=[bk.deps[0], modv.deps[0], xg.deps[s]], writes=[xn.deps[s2]])
                            kb.dma('sp', XT[u * 128:(u + 1) * 128, t0:t0 + w], xn.t[:, s2, 0:w], xn.sems[s2],
                                   reads=[xn.deps[s2]], writes=[XT_ds[u][hf]])
                    stream_units(st, KC, lambda u: w_out[l, u].rearrange("p (k c) -> p k c", k=KC), KC, body, depth=3, pre=pre, slots=4)

        with ExitStack() as st:
            xt = Tile(kb, st, "fxt", [128, KC, 256], F32, slots=2, dma=True)
            sq = Tile(kb, st, "fsq", [128, 256], F32, slots=3)
            rs = Tile(kb, st, "frs", [128, 256], F32, slots=2)
            rstd = Tile(kb, st, "frstd", [128, 256], F32, slots=2)
            yn = Tile(kb, st, "fyn", [128, KC, 256], F32, slots=2)
            ot = Tile(kb, st, "fot", [128, D], F32, slots=2, dma=True)
            for (t0, w) in token_groups(NCTX, NT, 256):
                s = xt.nxt()
                kb.dma('sp', xt.ap(s), fm(XT)[:, :, t0:t0 + w], xt.sems[s], reads=xtd(t0, t0 + w), writes=[xt.deps[s]])
                bk = next_bank()
                for c in range(KC):
                    q = sq.nxt()
                    kb.op('act', lambda e, c=c, q=q, s=s: e.activation(sq.ap(q), xt.t[:, s, c, :], AF.Square),
                          reads=[xt.deps[s]], writes=[sq.deps[q]])
                    kb.op('pe', lambda e, c=c, q=q: e.matmul(bk.t[:, 0, 0:w], ones_f, sq.ap(q), start=(c == 0), stop=(c == KC - 1)),
                          reads=[sq.deps[q]] + CD, writes=[bk.deps[0]], inc=True)
                r = rs.nxt()
                kb.op('act', lambda e: e.activation(rs.ap(r), bk.t[:, 0, 0:w], AF.Sqrt, bias=epsT.t[:, 0, :], scale=1.0 / D),
                      reads=[bk.deps[0], epsT.deps[0]], writes=[rs.deps[r]])
                r2 = rstd.nxt()
                kb.op('dve', lambda e: e.reciprocal(rstd.ap(r2), rs.ap(r)), reads=[rs.deps[r]], writes=[rstd.deps[r2]])
                y = yn.nxt()
                for c in range(KC):
                    kb.op('dve', lambda e, c=c: e.scalar_tensor_tensor(
                        yn.t[:, y, c, :], xt.t[:, s, c, :], vecs.t[:, 0, KC + 48 + c:KC + 48 + c + 1], rstd.ap(r2), ALU.mult, ALU.mult),
                        reads=[xt.deps[s], vecs.deps[0], rstd.deps[r2]], writes=[yn.deps[y]])
                for tt in range(w // 128):
                    o = ot.nxt()
                    for q in range(4):
                        bk2 = next_bank()
                        for j in range(4):
                            c = q * 4 + j
                            kb.op('pe', lambda e, c=c, j=j, bk2=bk2: e.transpose(
                                bk2.t[:, 0, j * 128:(j + 1) * 128], yn.t[:, y, c, tt * 128:(tt + 1) * 128], ident_f),
                                reads=[yn.deps[y]] + CD, writes=[bk2.deps[0]], inc=(j == 3))
                        if q % 2 == 0:
                            kb.op('dve', lambda e, q=q, bk2=bk2: e.tensor_copy(ot.t[:, o, q * 512:(q + 1) * 512], bk2.t[:, 0, :]),
                                  reads=[bk2.deps[0]], writes=[ot.deps[o]])
                        else:
                            kb.op('act', lambda e, q=q, bk2=bk2: e.activation(ot.t[:, o, q * 512:(q + 1) * 512], bk2.t[:, 0, :], AF.Copy),
                                  reads=[bk2.deps[0]], writes=[ot.deps[o]])
                    r0 = t0 - NCTX + tt * 128
                    kb.dma('sp', out_d[r0:r0 + 128, :], ot.ap(o), ot.sems[o], reads=[ot.deps[o]], writes=[])
            for o in range(2):
                kb.eng['sp'].wait_ge(ot.sems[o].h, ot.sems[o].n)
        for dd in [HN_d, PB_d, PF_d, YT_d, ACCT_d] + xtd(0, NT):
            kb.wait_all('sp', [dd])
        build.last_ninst = dict(kb.ninst)
    return nc


def _na_bias_tables(rpb):
    L = rpb.shape[0]
    classes = [(10, range(-2, 3)), (0, range(0, 4)), (1, range(-1, 3)), (30, range(-2, 2)), (31, range(-3, 1))]
    out = np.zeros((L, 4, 128, 21, 128), np.float32)
    p = np.arange(128)
    rin, col = p // 64, p % 64
    idx = 0
    for (t, dl) in classes:
        for d in dl:
            kr = 2 * (t + d) + rin
            qr = 2 * t + rin
            start = np.clip(qr - 4, 0, 56)
            vrow = (kr[:, None] >= start[None, :]) & (kr[:, None] < start[None, :] + 8)
            cstart = np.clip(col - 8, 0, 48)
            vcol = (col[:, None] >= cstart[None, :]) & (col[:, None] < cstart[None, :] + 16)
            roff = np.clip(kr[:, None] - qr[None, :] + 7, 0, 14)
            coff = np.clip(col[:, None] - col[None, :] + 15, 0, 30)
            g = rpb[:, :, roff, coff]
            out[:, :, :, idx, :] = np.where((vrow & vcol)[None, None], g, np.float32(-30000.0))
            idx += 1
    return out


def host_inputs(inp, cfg=None):
    cfg = cfg or {}
    WL = cfg.get('wlayers', DEPTH)
    f = np.float32
    cols_r = np.zeros((NU * 128,), np.int64)
    valid = np.zeros((NU * 128,), bool)
    for u, (_, idx, _) in enumerate(UNITS):
        cols_r[u * 128:u * 128 + len(idx)] = idx
        valid[u * 128:u * 128 + len(idx)] = True
    w_in = np.asarray(inp['w_in'], f)[:WL]
    def unitize(w, kc):
        Lw, K_, N_ = w.shape
        nb = N_ // 128
        return np.ascontiguousarray(w.reshape(Lw, kc, 128, nb, 128).transpose(0, 3, 2, 1, 4).reshape(Lw, nb, 128, kc * 128))
    w_in_r = unitize(np.where(valid[None, None, :], w_in[:, :, cols_r], f(0)), KC)
    w_in_m = unitize(w_in[:, :, O_MERGE:], KC)

    def colT(v, n):
        v = np.asarray(v, f)
        return np.ascontiguousarray(v.reshape(v.shape[0], n, 128).transpose(0, 2, 1))
    consts = np.zeros((128, 3, 128), f)
    consts[:, 0, :] = np.eye(128, dtype=f)
    consts[:, 1, :] = 1.0
    consts[:, 2, :] = np.eye(128, dtype=f)[::-1]
    invc = np.zeros((4, NT), f)
    for gi, w in enumerate((2, 4, 8, 16)):
        for (t0, n) in ((0, NCTX), (NCTX, NLAT)):
            t = np.arange(n)
            lo = np.maximum(t - w // 2, 0)
            hi = np.minimum(t + (w - 1 - w // 2), n - 1)
            invc[gi, t0:t0 + n] = (1.0 / (hi - lo + 1).astype(f)).astype(f)
    invc = np.ascontiguousarray(np.broadcast_to(invc[None], (128, 4, NT)))
    vecs2 = np.zeros((WL, 128, 52), f)
    vecs2[:, :, 0:4] = colT(np.asarray(inp['pool_scale'])[:WL], 4)
    vecs2[:, :, 4:8] = colT(np.asarray(inp['mla_gq'])[:WL], 4)
    vecs2[:, :, 8:12] = colT(np.asarray(inp['mla_gkv'])[:WL], 4)
    vecs2[:, :, 12:16] = colT(np.asarray(inp['ml_gnorm'])[:WL], 4)
    cw = np.asarray(inp['conv_w'], f)[:WL]
    vecs2[:, :, 16:48] = cw.reshape(WL, 4, 8, 128).transpose(0, 3, 2, 1).reshape(WL, 128, 32)
    bif = np.asarray(inp['b_if'], f)[:WL]
    for k, (dr, ty) in enumerate(((0, 0), (1, 0), (0, 1), (1, 1))):
        vecs2[:, 0:4, 48 + k] = bif[:, dr * 8 + ty * 4:dr * 8 + ty * 4 + 4]
    wuq = np.asarray(inp['w_uq'], f)[:WL]
    perm = _rope_perm()
    cols = []
    for h in range(4):
        b = h * 192
        cols += list(b + np.arange(128)) + list(b + 128 + np.arange(64)) + list(b + 128 + perm) + list(b + 128 + np.arange(64))
    w_uq_ext = np.ascontiguousarray(wuq[:, :, np.array(cols)])
    pos = np.arange(NLAT)
    rows, colsp = (pos // 64).astype(f), (pos % 64).astype(f)
    inv = (np.float32(10000.0) ** (-np.arange(16, dtype=f) / np.float32(16))).astype(f)
    ropeT = np.zeros((64, 2, NLAT), f)
    for i in range(64):
        half, j = i // 32, i % 32
        ang = ((rows if half == 0 else colsp) * inv[j % 16]).astype(f)
        ropeT[i, 0] = np.cos(ang).astype(f)
        ropeT[i, 1] = (-np.sin(ang) if j < 16 else np.sin(ang)).astype(f)
    pp = np.arange(128)[:, None]
    ff = np.arange(512)[None, :]
    dmask = np.zeros((128, 8, 512), f)
    for k in range(4):
        dmask[:, k, :] = (ff - pp >= 128 * k)
        dmask[:, 4 + k, :] = (ff - pp <= 128 * k)
    sel = np.zeros((128, 4, 128), f)
    for h in range(4):
        sel[h, h, :] = 1.0
    shared = {
        'w_mod': unitize(np.asarray(inp['w_mod'], f)[:WL], KC),
        'b_modT': colT(np.asarray(inp['b_mod'])[:WL], 48),
        'norm_gT': colT(np.asarray(inp['norm_g'])[:WL], KC),
        'w_in_r': w_in_r,
        'w_in_m': w_in_m,
        'w_br': np.stack([unitize(np.asarray(inp['w_br'], f)[:WL, i], 4) for i in range(4)], 1),
        'w_out': unitize(np.asarray(inp['w_out'], f)[:WL], KC),
        'g_finalT': colT(np.asarray(inp['g_final'])[None], KC)[0],
        'consts': consts,
        'invc': invc,
        'pool_w': np.ascontiguousarray(np.asarray(inp['pool_w'], f)[:WL]),
        'vecs2': vecs2,
        'w_uq_ext': w_uq_ext,
        'w_ukv': np.ascontiguousarray(np.asarray(inp['w_ukv'], f)[:WL]),
        'ropeT': ropeT,
        'na_bias': _na_bias_tables(np.asarray(inp['na_rpb'], f)[:WL]),
        'dmask': dmask.astype(ml_dtypes.bfloat16),
        'sel': sel,
    }
    maps = []
    for core in range(8):
        b = core % 4
        cc = np.stack([np.asarray(inp['c'], f)[b], np.asarray(inp['c_ctx'], f)], 0)
        ccT = np.ascontiguousarray(cc.reshape(2, KC, 128).transpose(2, 1, 0))
        m = dict(shared)
        m['x'] = np.ascontiguousarray(inp['x'][b], f)
        m['ctx'] = np.ascontiguousarray(inp['ctx'][b], f)
        m['ccT'] = ccT
        maps.append(m)
    return maps


def kernel(**inputs):
    nc = build({})
    maps = host_inputs(inputs)
    res = run_bass_kernel_spmd(nc, maps, core_ids=list(range(8)))
    return np.stack([np.asarray(res.results[b]['out'], np.float32) for b in range(4)], 0)
```
